# Optimizing a Trainium2 kernel written in Bass

```python
import jax, jax.numpy as jnp
from jax import lax
import numpy as np

D_MODEL = 2048
BATCH = 1
SEQ = 16384
DEPTH = 1

HEAD_DIM = 128
ATTN_GROUPS = ((128, 1), (512, 4), (2048, 16))
HEADS_PER_GROUP = 4
N_ATTN_HEADS = HEADS_PER_GROUP * len(ATTN_GROUPS)
ATTN_WIDTH = N_ATTN_HEADS * HEAD_DIM
ATTN_OUT_WIDTH = HEADS_PER_GROUP * HEAD_DIM
ROPE_THETA = 10000.0
POOL_WINDOWS = (2, 4, 8, 16)
POOL_GROUP_DIM = D_MODEL // 8
POOL_WIDTH = len(POOL_WINDOWS) * POOL_GROUP_DIM
N_MEM = 256
MEM_HEADS = 4
MEM_HEAD_DIM = D_MODEL // 8
MEM_WIDTH = MEM_HEADS * MEM_HEAD_DIM
N_BRANCHES = 3
IN_COLS = POOL_WIDTH + 3 * ATTN_WIDTH + MEM_WIDTH + N_BRANCHES * D_MODEL
D_FF = 5632
EPS = 1e-6
NEG = -1e30

kernel_name = "hybrid_gated_pool_dilated_mem_encoder"


def rmsnorm(x, g):
    xf = x.astype(jnp.float32)
    y = xf * lax.rsqrt(jnp.mean(xf * xf, axis=-1, keepdims=True) + EPS)
    return (y * g.astype(jnp.float32)).astype(x.dtype)


def swiglu(x, w_gate, w_up, w_down):
    return (jax.nn.silu(x @ w_gate) * (x @ w_up)) @ w_down


def rope(t, positions):
    dh = t.shape[-1]
    half = dh // 2
    inv_freq = ROPE_THETA ** (-jnp.arange(half, dtype=jnp.float32) / half)
    ang = positions.astype(jnp.float32)[..., None] * inv_freq
    cos = jnp.cos(ang)[:, :, None, :]
    sin = jnp.sin(ang)[:, :, None, :]
    tf = t.astype(jnp.float32)
    t1, t2 = tf[..., :half], tf[..., half:]
    out = jnp.concatenate([t1 * cos - t2 * sin, t2 * cos + t1 * sin], axis=-1)
    return out.astype(t.dtype)


def band_attention(q, k, v, key_valid, half_span):
    B, N, H, Dh = q.shape
    blk = half_span
    nb = N // blk

    def with_halo(t):
        t = t.reshape((B, nb, blk) + t.shape[2:])
        tp = jnp.pad(t, [(0, 0), (1, 1)] + [(0, 0)] * (t.ndim - 2))
        return jnp.concatenate([tp[:, :-2], tp[:, 1:-1], tp[:, 2:]], axis=2)

    qb = q.reshape(B, nb, blk, H, Dh)
    kb, vb, vk = with_halo(k), with_halo(v), with_halo(key_valid)
    s = jnp.einsum('bnqhd,bnkhd->bnhqk', qb, kb).astype(jnp.float32) * (Dh ** -0.5)
    rel = jnp.arange(3 * blk)[None, :] - blk - jnp.arange(blk)[:, None]
    allowed = (jnp.abs(rel) <= half_span)[None, None, None] & vk[:, :, None, None, :]
    s = jnp.where(allowed, s, NEG)
    lse = jax.nn.logsumexp(s, axis=-1)
    p = jnp.exp(s - lse[..., None]).astype(v.dtype)
    o = jnp.einsum('bnhqk,bnkhd->bnqhd', p, vb).reshape(B, N, H, Dh)
    lse = jnp.transpose(lse, (0, 1, 3, 2)).reshape(B, N, H)
    return o, lse


def dilated_window_group(q, k, v, window, dilation):
    B, S, H, Dh = q.shape
    half_span = window // (2 * dilation)
    unit = dilation * half_span
    Sp = -(-S // unit) * unit
    L = Sp // dilation
    valid = jnp.broadcast_to(jnp.arange(Sp) < S, (B, Sp))

    def to_sub(t):
        if t.shape[1] != Sp:
            t = jnp.pad(t, [(0, 0), (0, Sp - t.shape[1])] + [(0, 0)] * (t.ndim - 2))
        t = jnp.moveaxis(t.reshape((B, L, dilation) + t.shape[2:]), 2, 1)
        return t.reshape((B * dilation, L) + t.shape[3:])

    def from_sub(t):
        t = jnp.moveaxis(t.reshape((B, dilation, L) + t.shape[2:]), 1, 2)
        return t.reshape((B, Sp) + t.shape[3:])[:, :S]

    o, lse = band_attention(to_sub(q), to_sub(k), to_sub(v), to_sub(valid), half_span)
    return from_sub(o), from_sub(lse)


def dilated_attention_branch(q, k, v, positions):
    B, S, _ = q.shape
    q = rope(q.reshape(B, S, N_ATTN_HEADS, HEAD_DIM), positions)
    k = rope(k.reshape(B, S, N_ATTN_HEADS, HEAD_DIM), positions)
    v = v.reshape(B, S, N_ATTN_HEADS, HEAD_DIM)
    outs, lses = [], []
    for g, (window, dilation) in enumerate(ATTN_GROUPS):
        hs = slice(g * HEADS_PER_GROUP, (g + 1) * HEADS_PER_GROUP)
        o, lse = dilated_window_group(q[:, :, hs], k[:, :, hs], v[:, :, hs], window, dilation)
        outs.append(o)
        lses.append(lse)
    w = jax.nn.softmax(jnp.stack(lses, axis=0), axis=0)
    o = jnp.sum(w[..., None].astype(q.dtype) * jnp.stack(outs, axis=0), axis=0)
    return o.reshape(B, S, ATTN_OUT_WIDTH)


def centred_mean_minus_self(u, window):
    B, S, C = u.shape
    half = window // 2
    cs = jnp.concatenate([jnp.zeros((B, 1, C), u.dtype), lax.cumsum(u, axis=1)], axis=1)
    idx = jnp.arange(S)
    lo = jnp.clip(idx - half, 0, S)
    hi = jnp.clip(idx + half + 1, 0, S)
    cnt = (hi - lo).astype(u.dtype)
    return (cs[:, hi] - cs[:, lo]) / cnt[None, :, None] - u


def pooling_branch(u, pool_w, pool_scale):
    B, S, _ = u.shape
    uf = u.astype(jnp.float32).reshape(B, S, len(POOL_WINDOWS), POOL_GROUP_DIM)
    pooled = jnp.stack([centred_mean_minus_self(uf[:, :, g], w)
                        for g, w in enumerate(POOL_WINDOWS)], axis=2).astype(u.dtype)
    mixed = jnp.einsum('bsgc,gcd->bsgd', pooled, pool_w).reshape(B, S, POOL_WIDTH)
    return mixed * pool_scale


def memory_branch(q_mem, mem_n, w_mem_kv):
    B, S, _ = q_mem.shape
    kv = mem_n @ w_mem_kv
    k_m = kv[..., :MEM_WIDTH].reshape(B, -1, MEM_HEADS, MEM_HEAD_DIM)
    v_m = kv[..., MEM_WIDTH:].reshape(B, -1, MEM_HEADS, MEM_HEAD_DIM)
    q = q_mem.reshape(B, S, MEM_HEADS, MEM_HEAD_DIM)
    s = jnp.einsum('bshd,bmhd->bhsm', q, k_m).astype(jnp.float32) * (MEM_HEAD_DIM ** -0.5)
    p = jax.nn.softmax(s, axis=-1).astype(v_m.dtype)
    return jnp.einsum('bhsm,bmhd->bshd', p, v_m).reshape(B, S, MEM_WIDTH)


def setup_inputs(seed: int = 0) -> dict:
    key = jax.random.key(seed)
    ks = iter(jax.random.split(key, 32))
    L = DEPTH

    def w(shape, fan_in):
        return jax.random.normal(next(ks), shape, jnp.float32) * fan_in ** -0.5

    def gain(shape):
        return 1.0 + 0.02 * jax.random.normal(next(ks), shape, jnp.float32)

    return {
        "x": jax.random.normal(next(ks), (BATCH, SEQ, D_MODEL), jnp.float32),
        "mem": jax.random.normal(next(ks), (BATCH, N_MEM, D_MODEL), jnp.float32),
        "positions": jnp.broadcast_to(jnp.arange(SEQ, dtype=jnp.int32), (BATCH, SEQ)),
        "ffn1_norm": gain((L, D_MODEL)),
        "ffn1_w_gate": w((L, D_MODEL, D_FF), D_MODEL),
        "ffn1_w_up": w((L, D_MODEL, D_FF), D_MODEL),
        "ffn1_w_down": w((L, D_FF, D_MODEL), D_FF),
        "mix_norm": gain((L, D_MODEL)),
        "w_in": w((L, D_MODEL, IN_COLS), D_MODEL),
        "b_gate": 0.01 * jax.random.normal(next(ks), (L, N_BRANCHES * D_MODEL), jnp.float32),
        "pool_w": w((L, len(POOL_WINDOWS), POOL_GROUP_DIM, POOL_GROUP_DIM), POOL_GROUP_DIM),
        "pool_scale": gain((L, POOL_WIDTH)),
        "mem_norm": gain((L, D_MODEL)),
        "w_mem_kv": w((L, D_MODEL, 2 * MEM_WIDTH), D_MODEL),
        "w_up_pool": w((L, POOL_WIDTH, D_MODEL), POOL_WIDTH),
        "w_up_attn": w((L, ATTN_OUT_WIDTH, D_MODEL), ATTN_OUT_WIDTH),
        "w_up_mem": w((L, MEM_WIDTH, D_MODEL), MEM_WIDTH),
        "w_out": w((L, D_MODEL, D_MODEL), D_MODEL),
        "ffn2_norm": gain((L, D_MODEL)),
        "ffn2_w_gate": w((L, D_MODEL, D_FF), D_MODEL),
        "ffn2_w_up": w((L, D_MODEL, D_FF), D_MODEL),
        "ffn2_w_down": w((L, D_FF, D_MODEL), D_FF),
        "final_norm": gain((D_MODEL,)),
    }


def reference(x, mem, positions, ffn1_norm, ffn1_w_gate, ffn1_w_up, ffn1_w_down,
              mix_norm, w_in, b_gate, pool_w, pool_scale, mem_norm, w_mem_kv,
              w_up_pool, w_up_attn, w_up_mem, w_out,
              ffn2_norm, ffn2_w_gate, ffn2_w_up, ffn2_w_down, final_norm):
    c1 = POOL_WIDTH
    c2 = c1 + ATTN_WIDTH
    c3 = c2 + ATTN_WIDTH
    c4 = c3 + ATTN_WIDTH
    c5 = c4 + MEM_WIDTH
    for l in range(DEPTH):
        x = x + 0.5 * swiglu(rmsnorm(x, ffn1_norm[l]), ffn1_w_gate[l], ffn1_w_up[l], ffn1_w_down[l])
        h = rmsnorm(x, mix_norm[l])
        proj = h @ w_in[l]
        u_pool = proj[..., :c1]
        q, k, v = proj[..., c1:c2], proj[..., c2:c3], proj[..., c3:c4]
        q_mem = proj[..., c4:c5]
        gates = jax.nn.sigmoid(proj[..., c5:] + b_gate[l])
        y_pool = pooling_branch(u_pool, pool_w[l], pool_scale[l]) @ w_up_pool[l]
        y_attn = dilated_attention_branch(q, k, v, positions) @ w_up_attn[l]
        y_mem = memory_branch(q_mem, rmsnorm(mem, mem_norm[l]), w_mem_kv[l]) @ w_up_mem[l]
        merged = (gates[..., :D_MODEL] * y_pool
                  + gates[..., D_MODEL:2 * D_MODEL] * y_attn
                  + gates[..., 2 * D_MODEL:] * y_mem)
        x = x + merged @ w_out[l]
        x = x + 0.5 * swiglu(rmsnorm(x, ffn2_norm[l]), ffn2_w_gate[l], ffn2_w_up[l], ffn2_w_down[l])
    return rmsnorm(x, final_norm)
```

```python
import os
from contextlib import ExitStack
import numpy as np
import concourse.bass as bass
import concourse.mybir as mybir
from concourse.bass_utils import run_bass_kernel_spmd

F32 = mybir.dt.float32
BF16 = mybir.dt.bfloat16
I32 = mybir.dt.int32
ALU = mybir.AluOpType
AF = mybir.ActivationFunctionType

NCORE = 8
S = 16384
D = 2048
TPC = S // NCORE
T = 512
NT = TPC // T
KC = D // 128
DFF = 5632
FC = DFF // 128
HF = FC // 2
BLK = 8192
GD = (1, 4, 16)
EPS = 1e-6
DEBUG = bool(int(os.environ.get("KDEBUG", "0")))
STOP_AFTER = int(os.environ.get("KSTOP", "9"))

B_FFN1 = 0
B_WIN1 = 38
B_MEMKV = 49
B_QMEM = 53
B_POOLW = 55
B_GU3 = 56
B_WOUT = 88
B_FFN2 = 92
NBLK = 130


def blk_used(b):
    if B_FFN1 <= b < B_FFN1 + 38 or B_FFN2 <= b < B_FFN2 + 38:
        r = (b - (B_FFN1 if b < 38 else B_FFN2)) % 19
        return 8192 if r < 11 else 22 * 256
    if b == B_POOLW:
        return 2048
    if B_GU3 <= b < B_GU3 + 32:
        return 16 * 384 if (b - B_GU3) % 2 == 0 else 20 * 128
    return 8192

C_GAIN = 0
C_BG = 80
C_PS = 128
C_INVF = 136
C_SSIGN = 137
C_FPREV = 138
C_FNEXT = 146
C_VPREV = 154
C_VNEXT = 155
C_EPS = 156
C_ICNT = 160
NCF = 160 + 1536
CB_ID = 0
CB_MA = 128
CB_MB = 256
CB_BAND = 384
NCB = 384 + 4 * 3 * 3 * 128

PKG_G = 128 * 4 * 64
PKG_RR0 = (0, 1, 5)
PKG_U = 4 * 21 * PKG_G
PKG_N = PKG_U + 2 * 128 * 1024
PKG_ROWS = PKG_N // 256


def _ffn_blocks(wg, wu, wd):
    out = []
    for j in range(22):
        b = np.concatenate([wg[:, 256 * j:256 * j + 256], wu[:, 256 * j:256 * j + 256]], axis=1)
        out.append(b.reshape(16, 128, 512).transpose(1, 0, 2).reshape(128, 8192))
    for half in range(2):
        for n2 in range(8):
            b = wd[half * 2816:(half + 1) * 2816, n2 * 256:(n2 + 1) * 256]
            bb = np.zeros((128, BLK), np.float32)
            bb[:, :22 * 256] = b.reshape(22, 128, 256).transpose(1, 0, 2).reshape(128, 22 * 256)
            out.append(bb)
    res = []
    for half in range(2):
        res += out[half * 11:(half + 1) * 11] + out[22 + half * 8:22 + (half + 1) * 8]
    return res


def _kblock(w):
    K, C = w.shape
    kc = K // 128
    bb = np.zeros((128, BLK), np.float32)
    bb[:, :kc * C] = w.reshape(kc, 128, C).transpose(1, 0, 2).reshape(128, kc * C)
    return bb


def build_wall(inp):
    f1 = _ffn_blocks(inp["ffn1_w_gate"][0], inp["ffn1_w_up"][0], inp["ffn1_w_down"][0])
    f2 = _ffn_blocks(inp["ffn2_w_gate"][0], inp["ffn2_w_up"][0], inp["ffn2_w_down"][0])
    win = inp["w_in"][0]
    blocks = []
    blocks += f1
    blocks.append(_kblock(win[:, 0:512]))
    blocks.append(_kblock(win[:, 512:1024]))
    for g in range(3):
        for t in range(3):
            c0 = 1024 + t * 1536 + g * 512
            blocks.append(_kblock(win[:, c0:c0 + 512]))
    wkv = inp["w_mem_kv"][0]
    for j in range(4):
        blocks.append(_kblock(wkv[:, j * 512:(j + 1) * 512]))
    blocks.append(_kblock(win[:, 5632:6144]))
    blocks.append(_kblock(win[:, 6144:6656]))
    pw = inp["pool_w"][0]
    bb = np.zeros((128, BLK), np.float32)
    bb[:, :8 * 256] = pw.reshape(4, 2, 128, 256).transpose(2, 0, 1, 3).reshape(128, 8 * 256)
    blocks.append(bb)
    wup = np.concatenate([inp["w_up_pool"][0], inp["w_up_attn"][0], inp["w_up_mem"][0]], axis=0)
    for c in range(16):
        cols = np.concatenate([win[:, 6656 + i * 2048 + c * 128: 6656 + i * 2048 + (c + 1) * 128] for i in range(3)], axis=1)
        blocks.append(_kblock(cols))
        blocks.append(_kblock(wup[:, c * 128:(c + 1) * 128]))
    wo = inp["w_out"][0]
    for j in range(4):
        blocks.append(_kblock(wo[:, j * 512:(j + 1) * 512]))
    blocks += f2
    assert len(blocks) == NBLK
    return np.ascontiguousarray(np.stack(blocks, 0).reshape(NBLK * 128, BLK))


def build_consts(inp, c):
    cf = np.zeros((128, NCF), np.float32)
    for i, nm in enumerate(["ffn1_norm", "mix_norm", "ffn2_norm", "final_norm", "mem_norm"]):
        v = np.asarray(inp[nm], np.float32).reshape(-1)
        cf[:, C_GAIN + 16 * i:C_GAIN + 16 * (i + 1)] = v.reshape(16, 128).T
    cf[:, C_BG:C_BG + 48] = np.asarray(inp["b_gate"], np.float32).reshape(48, 128).T
    cf[:, C_PS:C_PS + 8] = np.asarray(inp["pool_scale"], np.float32).reshape(8, 128).T
    half = 64
    invf = (10000.0 ** (-np.arange(half, dtype=np.float32) / half)).astype(np.float32)
    cf[:, C_INVF] = np.concatenate([invf, invf]) / np.float32(2 * np.pi)
    cf[:, C_EPS] = EPS
    cf[:64, C_SSIGN] = -1.0
    cf[64:, C_SSIGN] = 1.0
    if c > 0:
        cf[:, C_FPREV + c - 1] = 1.0
        cf[:, C_VPREV] = 1.0
    if c < NCORE - 1:
        cf[:, C_FNEXT + c + 1] = 1.0
        cf[:, C_VNEXT] = 1.0
    cb = np.zeros((128, NCB), np.float32)
    cb[:, CB_ID:CB_ID + 128] = np.eye(128, dtype=np.float32)
    i = np.arange(128)[:, None]
    j = np.arange(128)[None, :]
    cb[:, CB_MA:CB_MA + 128] = (i >= j)
    cb[:, CB_MB:CB_MB + 128] = (i <= j)
    for wg, w in enumerate((2, 4, 8, 16)):
        hw = w // 2
        for var, ti in enumerate((0, 1, 15)):
            g0 = c * TPC + ti * 128
            P = g0 + np.arange(128)
            lo = np.clip(P - hw, 0, S)
            hi = np.clip(P + hw + 1, 0, S)
            cnt = (hi - lo).astype(np.float32)
            cf[:, C_ICNT + (wg * 3 + var) * 128:C_ICNT + (wg * 3 + var + 1) * 128] = (1.0 / cnt)[None, :]
            for nb in range(3):
                Pp = g0 + (nb - 1) * 128 + np.arange(128)
                B = ((Pp[:, None] >= lo[None, :]) & (Pp[:, None] < hi[None, :])).astype(np.float32)
                B -= (Pp[:, None] == P[None, :]) * cnt[None, :]
                o = CB_BAND + ((wg * 3 + var) * 3 + nb) * 128
                cb[:, o:o + 128] = B
    return cf, cb


class Res:
    __slots__ = ("name", "w", "r", "dsem", "dkey", "dcount")

    def __init__(self, name):
        self.name = name
        self.w = []
        self.r = {}
        self.dsem = None
        self.dkey = None
        self.dcount = 0


class Prog:
    def __init__(self, nc, stack):
        self.nc = nc
        self.stack = stack
        self.sems = {}
        self.eng = {}
        self.allres = []
        for key, h in (("pe", nc.tensor), ("dve", nc.vector), ("act", nc.scalar), ("pool", nc.gpsimd), ("sp", nc.sync)):
            sem = stack.enter_context(nc.semaphore("e_" + key))
            self.sems[key] = sem
            self.eng[key] = {"h": h, "count": 0, "seen": {}}
        self.nsem = 0

    def res(self, name):
        r = Res(name)
        self.allres.append(r)
        return r

    def _dsem(self, r):
        if r.dsem is None:
            self.nsem += 1
            r.dkey = "d%d_%s" % (self.nsem, r.name)
            r.dsem = self.stack.enter_context(self.nc.semaphore(r.dkey))
            self.sems[r.dkey] = r.dsem
        return r.dsem

    def _wait(self, ek, toks):
        e = self.eng[ek]
        need = {}
        for (k, v) in toks:
            if k == ek and ek in ("pe",):
                continue
            if e["seen"].get(k, 0) < v:
                need[k] = max(need.get(k, 0), v)
        for k, v in need.items():
            e["h"].wait_ge(self.sems[k], v)
            e["seen"][k] = v

    @staticmethod
    def _deps(reads, writes):
        toks = []
        for r in reads:
            toks += r.w
        for w in writes:
            toks += w.w
            toks += list(w.r.items())
        return toks

    @staticmethod
    def _commit(reads, writes, tok):
        for r in reads:
            if r.r.get(tok[0], 0) < tok[1]:
                r.r[tok[0]] = tok[1]
        for w in writes:
            w.w = [tok]
            w.r = {}

    def op(self, ek, reads, writes, fn):
        self._wait(ek, self._deps(reads, writes))
        e = self.eng[ek]
        ins = fn(e["h"])
        e["count"] += 1
        ins.then_inc(self.sems[ek], 1)
        self._commit(reads, writes, (ek, e["count"]))

    def dma(self, qk, reads, writes, semres, out_ap, in_ap):
        self._wait(qk, self._deps(reads, writes))
        sem = self._dsem(semres)
        ins = self.eng[qk]["h"].dma_start(out=out_ap, in_=in_ap)
        semres.dcount += 16
        ins.then_inc(sem, 16)
        self._commit(reads, writes, (semres.dkey, semres.dcount))

    def barrier(self):
        toks = [(k, e["count"]) for k, e in self.eng.items() if e["count"] > 0 and k != "sp"]
        for r in self.allres:
            if r.dsem is not None and r.dcount > 0:
                toks.append((r.dkey, r.dcount))
        for ek in self.eng:
            e = self.eng[ek]
            for (k, v) in toks:
                if k == ek:
                    continue
                if e["seen"].get(k, 0) < v:
                    e["h"].wait_ge(self.sems[k], v)
                    e["seen"][k] = v
        for r in self.allres:
            r.w = [t for t in r.w if t[0] in ("dve", "act", "pool")]
            r.r = {k: v for k, v in r.r.items() if k in ("dve", "act", "pool")}


def build_program():
    nc = bass.Bass("TRN2", target_bir_lowering=False)
    dk = "ExternalOutput" if DEBUG else "Internal"
    xT = nc.dram_tensor("xT", [D, TPC], F32, kind="ExternalInput").ap()
    wall = nc.dram_tensor("wall", [NBLK * 128, BLK], F32, kind="ExternalInput").ap()
    cfd = nc.dram_tensor("cf", [128, NCF], F32, kind="ExternalInput").ap()
    cbd = nc.dram_tensor("cb", [128, NCB], F32, kind="ExternalInput").ap()
    posd = nc.dram_tensor("pos", [128, TPC], I32, kind="ExternalInput").ap()
    memTd = nc.dram_tensor("memT", [D, 256], F32, kind="ExternalInput").ap()
    outT = nc.dram_tensor("outT", [D, TPC], F32, kind="ExternalOutput").ap()
    WSPLIT = 64
    wbf_a = nc.dram_tensor("wbf_a", [WSPLIT * 128, BLK], BF16, kind="Internal").ap()
    wbf_b = nc.dram_tensor("wbf_b", [(NBLK - WSPLIT) * 128, BLK], BF16, kind="Internal").ap()

    def wbf_rows(b0, b1):
        if b1 <= WSPLIT:
            return wbf_a[b0 * 128:b1 * 128, :]
        assert b0 >= WSPLIT
        return wbf_b[(b0 - WSPLIT) * 128:(b1 - WSPLIT) * 128, :]
    x1s = nc.dram_tensor("x1s", [D, TPC], F32, kind=dk).ap()
    us = nc.dram_tensor("us", [16 * 128, 1024], BF16, kind=dk).ap()
    aos = nc.dram_tensor("aos", [128, 4 * TPC], BF16, kind=dk).ap()
    LH = [TPC // d + 128 for d in GD]
    LQ = [TPC // d for d in GD]
    qkvs = [[nc.dram_tensor("qkv%d_%d" % (t, g), [128, 4 * GD[g] * (LQ[g] if t == 0 else LH[g])], BF16, kind=dk).ap() for g in range(3)] for t in range(3)]
    pkg = nc.dram_tensor("pkg", [PKG_ROWS, 256], BF16, kind="Internal").ap()
    gat = nc.dram_tensor("gat", [NCORE * PKG_ROWS, 256], BF16, kind="Internal").ap()

    with ExitStack() as stack:
        arena = stack.enter_context(nc.sbuf_tensor("arena", [128, 103 * 1024 + 384], BF16))
        ps = stack.enter_context(nc.psum_tensor("ps", [128, 4096], F32))
        P = Prog(nc, stack)
        cc_sem = stack.enter_context(nc.semaphore("cc_sem"))
        P.sems["cc"] = cc_sem

        state = {"off": 0}

        def alloc(nbytes):
            o = state["off"]
            state["off"] = o + ((nbytes + 63) // 64) * 64
            assert state["off"] <= (103 * 1024 + 384) * 2, state["off"]
            return o

        def view(off, nbytes, dt=BF16):
            a = arena[:, off // 2:(off + nbytes) // 2]
            if dt is not BF16:
                a = a.bitcast(dt)
            return a

        o_cf = alloc(NCF * 4)
        cf = view(o_cf, NCF * 4, F32)
        o_cb = alloc(NCB * 2)
        cb = view(o_cb, NCB * 2)
        o_misc = alloc(128 * 2 * 3 + 2048)
        ones = view(o_misc, 256)
        ones_f = view(o_misc + 256, 256)
        ones_l = view(o_misc + 512, 256)
        mask4 = view(o_misc + 768, 2048)
        o_uh = alloc(4096)
        uhalo = view(o_uh, 4096).rearrange("p (a c) -> p a c", a=2)
        o_km = alloc(8192)
        kmT = view(o_km, 4096).rearrange("p (a c) -> p a c", a=8)
        vm = view(o_km + 4096, 4096).rearrange("p (a c) -> p a c", a=2)
        R_const = P.res("const")
        R_uh = P.res("uhalo")
        R_km = P.res("kmvm")
        phase_mark = state["off"]

        ident = cb[:, CB_ID:CB_ID + 128]

        PB = [P.res("pb%d" % i) for i in range(8)]
        pbi = {"i": 0}

        STB = 7

        def bank():
            i = pbi["i"]
            pbi["i"] = (i + 1) % 7
            return PB[i], ps[:, i * 512:(i + 1) * 512]

        def bank2():
            i = pbi["i"]
            if i % 2:
                i += 1
            if i >= 6:
                i = 0
            pbi["i"] = (i + 2) % 7
            return [PB[i], PB[i + 1]], ps[:, i * 512:(i + 2) * 512]

        def mm(bres, out_ap, pairs, reads):
            def fn(h):
                n = len(pairs)
                ins = None
                for i, (l, r) in enumerate(pairs):
                    ins = h.matmul(out_ap, lhsT=l, rhs=r, start=(i == 0), stop=(i == n - 1))
                return ins
            P.op("pe", reads, bres if isinstance(bres, list) else [bres], fn)

        conv_res = [Res("conv%d" % i) for i in range(8)]
        blk_tok = {}
        chunks = [(b, b + 1) for b in range(NBLK)]
        conv_state = {"ci": 0}

        def issue_conv():
            ci = conv_state["ci"]
            if ci >= len(chunks):
                return False
            b0, b1 = chunks[ci]
            r = conv_res[ci % 8]
            sem = P._dsem(r)
            ins = nc.gpsimd.dma_start(out=wbf_rows(b0, b1)[:, 0:blk_used(b0)], in_=wall[b0 * 128:b1 * 128, 0:blk_used(b0)])
            r.dcount += 16
            ins.then_inc(sem, 16)
            for bb in range(b0, b1):
                blk_tok[bb] = (r.dkey, r.dcount)
            conv_state["ci"] = ci + 1
            return True

        def conv_upto(blk, paced=True):
            while conv_state["ci"] < len(chunks) and chunks[conv_state["ci"]][0] <= blk:
                if paced:
                    P._wait("pool", [("pe", P.eng["pe"]["count"])])
                issue_conv()

        conv_upto(7, paced=False)

        NSLOT = 3
        wsl = {"n": 0, "sched": [], "pos": 0, "issued": 0}

        def setup_wslots():
            offs = [alloc(BLK * 2) for _ in range(NSLOT)]
            wsl["ap"] = [view(o, BLK * 2) for o in offs]
            wsl["res"] = [P.res("wslot%d" % i) for i in range(NSLOT)]

        def _issue_next():
            i = wsl["issued"]
            if i >= len(wsl["sched"]):
                return
            blk = wsl["sched"][i]
            s = i % NSLOT
            r = wsl["res"][s]
            P._wait("sp", [blk_tok[blk]])
            P.dma("sp", [], [r], r, wsl["ap"][s][:, 0:blk_used(blk)], wbf_rows(blk, blk + 1)[:, 0:blk_used(blk)])
            wsl["issued"] = i + 1

        def wstart(sched):
            wsl["sched"] = sched
            wsl["pos"] = 0
            wsl["issued"] = 0
            for _ in range(NSLOT - 1):
                _issue_next()

        def wnext(expect):
            i = wsl["pos"]
            assert wsl["sched"][i] == expect, (i, wsl["sched"][i], expect)
            while wsl["issued"] < min(i + NSLOT, len(wsl["sched"])):
                if wsl["issued"] - i >= NSLOT:
                    break
                _issue_next()
            wsl["pos"] = i + 1
            s = i % NSLOT
            if wsl.get("pace"):
                if i < 49 and wsl["pace"] == 1:
                    conv_upto(min(wsl["sched"][i] + 10, 52))
                elif wsl["pace"] == 3:
                    conv_upto(min(wsl["sched"][i] + 10, NBLK - 1))
                elif (i - 49) % 2 == 0 and conv_state["ci"] < len(chunks) and chunks[conv_state["ci"]][0] < 116:
                    P._wait("pool", [("pe", P.eng["pe"]["count"])])
                    issue_conv()
            return wsl["res"][s], wsl["ap"][s]

        def setup_common(ncp=64):
            d = {}
            o = alloc(KC * T * 4)
            d["xs"] = view(o, KC * T * 4, F32).rearrange("p (k t) -> p k t", k=KC)
            d["XS"] = [P.res("xs%d" % k) for k in range(KC)]
            o = alloc(ncp * T * 2)
            d["cp"] = view(o, ncp * T * 2).rearrange("p (k t) -> p k t", k=ncp)
            d["CP"] = [P.res("cp%d" % k) for k in range(ncp)]
            o = alloc(T * 4)
            d["rstd"] = view(o, T * 4, F32)
            d["Rrstd"] = P.res("rstd")
            d["tmp"] = []
            d["Rtmp"] = []
            for i in range(4):
                o = alloc(T * 4)
                d["tmp"].append(view(o, T * 4, F32))
                d["Rtmp"].append(P.res("tmp%d" % i))
            d["ti"] = 0
            return d

        def tmpf(cm):
            i = cm["ti"]
            cm["ti"] = (i + 1) % 4
            return cm["Rtmp"][i], cm["tmp"][i]

        def stats_chunk(cm, k, ncols=T):
            xs, XS, cp, CP = cm["xs"], cm["XS"], cm["cp"], cm["CP"]
            sl = slice(0, ncols)
            P.op("act", [XS[k]], [CP[k]], lambda h: h.activation(out=cp[:, k, sl], in_=xs[:, k, sl], func=AF.Square))
            P.op("pe", [CP[k], R_const], [PB[STB]], lambda h: h.matmul(ps[:, STB * 512:STB * 512 + ncols], lhsT=ones, rhs=cp[:, k, sl], start=(k == 0), stop=(k == KC - 1)))

        def norm_finish(cm, gcol, dst_fn, ncols=T, after_k=None):
            xs, XS = cm["xs"], cm["XS"]
            sl = slice(0, ncols)
            rstd, Rr = cm["rstd"], cm["Rrstd"]
            bap = ps[:, STB * 512:STB * 512 + ncols]
            P.op("act", [PB[STB], R_const], [Rr], lambda h: h.activation(out=rstd[:, sl], in_=bap, func=AF.Sqrt, scale=1.0 / D, bias=cf[:, C_EPS:C_EPS + 1]))
            P.op("dve", [], [Rr], lambda h: h.reciprocal(out=rstd[:, sl], in_=rstd[:, sl]))
            for k in range(KC):
                dr, dap = dst_fn(k)
                P.op("dve", [XS[k], Rr, R_const], [dr],
                     lambda h: h.scalar_tensor_tensor(out=dap, in0=xs[:, k, sl], scalar=cf[:, gcol + k:gcol + k + 1], in1=rstd[:, sl], op0=ALU.mult, op1=ALU.mult))
                if after_k is not None:
                    after_k(k)

        def rmsnorm(cm, gcol, dst_fn, ncols=T):
            for k in range(KC):
                stats_chunk(cm, k, ncols)
            norm_finish(cm, gcol, dst_fn, ncols)

        def ffn(cm, b_base, HID0):
            xs, XS, cp, CP = cm["xs"], cm["XS"], cm["cp"], cm["CP"]
            for half in range(2):
                for j in range(11):
                    jb = half * 11 + j
                    wr, wap = wnext(b_base + half * 19 + j)
                    w3 = wap.rearrange("p (k c) -> p k c", k=KC)
                    for sub in range(2):
                        gr, gap = bank()
                        mm(gr, gap, [(w3[:, k, sub * 128:(sub + 1) * 128], cp[:, k, :]) for k in range(KC)], [wr] + CP[0:KC])
                        ur, uap = bank()
                        mm(ur, uap, [(w3[:, k, 256 + sub * 128:256 + (sub + 1) * 128], cp[:, k, :]) for k in range(KC)], [wr] + CP[0:KC])
                        tr, tap = tmpf(cm)
                        P.op("act", [gr], [tr], lambda h, gap=gap, tap=tap: h.activation(out=tap, in_=gap, func=AF.Silu))
                        hc = HID0 + j * 2 + sub
                        P.op("dve", [tr, ur], [CP[hc]], lambda h, tap=tap, uap=uap, hc=hc: h.tensor_tensor(out=cp[:, hc, :], in0=uap, in1=tap, op=ALU.mult))
                for n2 in range(8):
                    wr, wap = wnext(b_base + half * 19 + 11 + n2)
                    w3 = wap[:, 0:HF * 256].rearrange("p (k c) -> p k c", k=HF)
                    for sub in range(2):
                        n = n2 * 2 + sub
                        yr, yap = bank()
                        mm(yr, yap, [(w3[:, k, sub * 128:(sub + 1) * 128], cp[:, HID0 + k, :]) for k in range(HF)], [wr] + CP[HID0:HID0 + HF])
                        P.op("dve", [yr], [XS[n]], lambda h, yap=yap, n=n: h.scalar_tensor_tensor(out=xs[:, n, :], in0=yap, scalar=0.5, in1=xs[:, n, :], op0=ALU.mult, op1=ALU.add))
                        if half == 1:
                            stats_chunk(cm, n)

        P.dma("sp", [], [R_const], R_const, cf, cfd)
        o_tmpcb = alloc(NCB * 4)
        cbtmp = view(o_tmpcb, NCB * 4, F32)
        R_cbt = P.res("cbtmp")
        P.dma("sp", [], [R_cbt], R_cbt, cbtmp, cbd)
        P.op("dve", [R_cbt], [R_const], lambda h: h.tensor_copy(out=cb, in_=cbtmp))
        P.op("dve", [], [R_const], lambda h: h.memset(ones, 1.0))
        P.op("dve", [], [R_const], lambda h: h.memset(ones_f, 1.0))
        P.op("dve", [], [R_const], lambda h: h.memset(ones_l, 1.0))
        P.op("dve", [], [R_const], lambda h: h.tensor_scalar(out=ones_f[0:64, :], in0=ones_f[0:64, :], scalar1=cf[0:64, C_VPREV:C_VPREV + 1], scalar2=None, op0=ALU.mult))
        P.op("dve", [], [R_const], lambda h: h.tensor_scalar(out=ones_l[64:128, :], in0=ones_l[64:128, :], scalar1=cf[64:128, C_VNEXT:C_VNEXT + 1], scalar2=None, op0=ALU.mult))
        m4 = mask4.rearrange("p (h a q) -> p h a q", h=4, a=2)
        for hh in range(4):
            P.op("dve", [], [R_const], lambda h, hh=hh: h.tensor_copy(out=m4[:, hh, :, :], in_=cb[:, CB_MA:CB_MA + 256].rearrange("p (a q) -> p a q", a=2)))
        P.barrier()
        state["off"] = phase_mark

        setup_wslots()
        cm = setup_common(38)
        xs, XS, cp, CP = cm["xs"], cm["XS"], cm["cp"], cm["CP"]
        HID0 = 16
        o = alloc(T * 4)
        posi = view(o, T * 4, I32)
        R_pos = P.res("pos")
        o = alloc(T * 4)
        cosT = view(o, T * 4, F32)
        o = alloc(T * 4)
        sinT = view(o, T * 4, F32)
        R_cs = P.res("cossin")
        o = alloc(T * 4)
        tint = view(o, T * 4, I32)
        R_ti = P.res("tint")
        stg = []
        Rstg = []
        for i in range(3):
            o = alloc(4 * T * 2)
            stg.append(view(o, 4 * T * 2))
            Rstg.append(P.res("stg%d" % i))
        o = alloc(4 * 1024 * 2)
        ustage = view(o, 4 * 1024 * 2).rearrange("p (a c) -> p a c", a=4)
        R_us = P.res("ustage")
        R_x1st = P.res("x1store")

        sched1 = []
        for m in range(NT):
            sched1 += [B_FFN1 + i for i in range(38)]
            sched1 += [B_WIN1 + i for i in range(11)]
        wstart(sched1)
        wsl["pace"] = 1
        xT3 = xT.rearrange("(k p) t -> p k t", p=128)
        x1s3 = x1s.rearrange("(k p) t -> p k t", p=128)
        outT3 = outT.rearrange("(k p) t -> p k t", p=128)
        stgi = 0
        for m in range(NT if STOP_AFTER >= 1 else 0):
            tsl = slice(m * T, (m + 1) * T)
            for k in range(KC):
                P.dma("sp", [], [XS[k]], XS[k], xs[:, k, :], xT3[:, k, tsl])
            rmsnorm(cm, C_GAIN + 0, lambda k: (CP[k], cp[:, k, :]))
            ffn(cm, B_FFN1, HID0)
            P.dma("pool", XS, [], R_x1st, x1s3[:, :, tsl], xs)
            norm_finish(cm, C_GAIN + 16, lambda k: (CP[k], cp[:, k, :]))
            t0r, t0 = tmpf(cm)
            t1r, t1 = tmpf(cm)
            P.dma("sp", [], [R_pos], R_pos, posi, posd[:, tsl])
            P.op("dve", [R_pos], [t0r], lambda h: h.tensor_copy(out=t0, in_=posi))
            P.op("dve", [R_const], [t0r], lambda h: h.tensor_scalar(out=t0, in0=t0, scalar1=cf[:, C_INVF:C_INVF + 1], scalar2=None, op0=ALU.mult))
            P.op("dve", [t0r], [R_ti], lambda h: h.tensor_copy(out=tint, in_=t0))
            P.op("dve", [R_ti], [t1r], lambda h: h.tensor_copy(out=t1, in_=tint))
            P.op("dve", [t1r], [t0r], lambda h: h.tensor_tensor(out=t0, in0=t0, in1=t1, op=ALU.subtract))

            def wrap(buf, bufr, scr, scrr):
                P.op("dve", [bufr], [scrr], lambda h: h.tensor_scalar(out=scr, in0=buf, scalar1=0.5, scalar2=None, op0=ALU.is_gt))
                P.op("dve", [scrr], [bufr], lambda h: h.tensor_tensor(out=buf, in0=buf, in1=scr, op=ALU.subtract))
                P.op("dve", [bufr], [scrr], lambda h: h.tensor_scalar(out=scr, in0=buf, scalar1=-0.5, scalar2=None, op0=ALU.is_lt))
                P.op("dve", [scrr], [bufr], lambda h: h.tensor_tensor(out=buf, in0=buf, in1=scr, op=ALU.add))
            wrap(t0, t0r, t1, t1r)
            P.op("act", [t0r], [R_cs], lambda h: h.activation(out=sinT, in_=t0, func=AF.Sin, scale=6.2831845))
            P.op("dve", [], [t0r], lambda h: h.tensor_scalar(out=t0, in0=t0, scalar1=0.25, scalar2=None, op0=ALU.add))
            wrap(t0, t0r, t1, t1r)
            P.op("act", [t0r], [R_cs], lambda h: h.activation(out=cosT, in_=t0, func=AF.Sin, scale=6.2831845))
            P.op("dve", [R_const], [R_cs], lambda h: h.tensor_scalar(out=sinT, in0=sinT, scalar1=cf[:, C_SSIGN:C_SSIGN + 1], scalar2=None, op0=ALU.mult))
            for bq in range(2):
                wr, wap = wnext(B_WIN1 + bq)
                w3 = wap.rearrange("p (k c) -> p k c", k=KC)
                for tt in range(4):
                    br, bap = bank()
                    mm(br, bap, [(cp[:, k, tt * 128:(tt + 1) * 128], w3[:, k, :]) for k in range(KC)], [wr] + CP[0:KC])
                    P.op("act", [br], [R_us], lambda h, bap=bap, tt=tt, bq=bq: h.activation(out=ustage[:, tt, bq * 512:(bq + 1) * 512], in_=bap, func=AF.Copy))
            P.dma("pool", [R_us], [], R_us, us[m * 512:(m + 1) * 512, :].rearrange("(a p) c -> p a c", p=128), ustage)
            for g in range(3):
                d = GD[g]
                L = T // d
                for t in range(3):
                    wr, wap = wnext(B_WIN1 + 2 + g * 3 + t)
                    w3 = wap.rearrange("p (k c) -> p k c", k=KC)
                    sr, sap = Rstg[stgi % 3], stg[stgi % 3]
                    stgi += 1
                    s4 = sap.rearrange("p (h r l) -> p h r l", h=4, r=d)
                    for hh in range(4):
                        br, bap = bank()
                        mm(br, bap, [(w3[:, k, hh * 128:(hh + 1) * 128], cp[:, k, :]) for k in range(KC)], [wr] + CP[0:KC])
                        bperm = bap.rearrange("p (l r) -> p r l", r=d)
                        if t == 2:
                            P.op("act", [br], [sr], lambda h, bperm=bperm, hh=hh, s4=s4: h.activation(out=s4[:, hh, :, :], in_=bperm, func=AF.Copy))
                        else:
                            ar, aap = tmpf(cm)
                            b2r, b2ap = tmpf(cm)
                            P.op("dve", [br, R_cs], [ar], lambda h, bap=bap, aap=aap: h.tensor_tensor(out=aap, in0=bap, in1=cosT, op=ALU.mult))
                            P.op("dve", [br, R_cs], [b2r], lambda h, bap=bap, b2ap=b2ap: h.tensor_tensor(out=b2ap[0:64, :], in0=bap[64:128, :], in1=sinT[0:64, :], op=ALU.mult))
                            P.op("dve", [br, R_cs], [b2r], lambda h, bap=bap, b2ap=b2ap: h.tensor_tensor(out=b2ap[64:128, :], in0=bap[0:64, :], in1=sinT[64:128, :], op=ALU.mult))
                            P.op("pool", [ar, b2r], [sr], lambda h, aap=aap, b2ap=b2ap, hh=hh, s4=s4, d=d: h.tensor_tensor(
                                out=s4[:, hh, :, :], in0=aap.rearrange("p (l r) -> p r l", r=d), in1=b2ap.rearrange("p (l r) -> p r l", r=d), op=ALU.add))
                    dst = qkvs[t][g].rearrange("p (h r l) -> p h r l", h=4, r=d)
                    off = (0 if t == 0 else 64) + m * L
                    if d == 16:
                        for hh in range(4):
                            P.dma("pool", [sr], [], sr, dst[:, hh, :, off:off + L], s4[:, hh, :, :])
                    else:
                        P.dma("pool", [sr], [], sr, dst[:, :, :, off:off + L], s4)
        wsl["pace"] = 0
        conv_upto(115)
        P.barrier()

        if STOP_AFTER >= 2:
            R_pk = P.res("pack")
            pkf = pkg.rearrange("r c -> (r c)")
            for ht in range(2):
                for t in range(2):
                    for g in range(3):
                        d = GD[g]
                        src = qkvs[1 + t][g].rearrange("p (h r l) -> p h r l", h=4, r=d)
                        l0 = 64 if ht == 0 else LH[g] - 128
                        o0 = ((ht * 2 + t) * 21 + PKG_RR0[g]) * PKG_G
                        dstv = pkf[o0:o0 + d * PKG_G].rearrange("(p h r l) -> p h r l", p=128, h=4, r=d)
                        for hh in range(4):
                            P.dma("pool", [], [], R_pk, dstv[:, hh, :, :], src[:, hh, :, l0:l0 + 64])
            P.dma("pool", [], [], R_pk, pkf[PKG_U:PKG_U + 131072].rearrange("(p c) -> p c", p=128), us[0:128, :])
            P.dma("pool", [], [], R_pk, pkf[PKG_U + 131072:PKG_U + 262144].rearrange("(p c) -> p c", p=128), us[15 * 128:16 * 128, :])
            P._wait("pool", [(R_pk.dkey, R_pk.dcount)])
            nc.gpsimd.collective_compute("AllGather", ALU.bypass, replica_groups=[list(range(NCORE))], ins=[pkg], outs=[gat]).then_inc(cc_sem, 1)
            for ek in P.eng:
                P.eng[ek]["h"].wait_ge(cc_sem, 1)
            P.barrier()

        if STOP_AFTER >= 3:
            state["off"] = phase_mark
            o_q = alloc(4 * 2048 * 2)
            o_k = alloc(4 * 4096 * 2)
            o_v = alloc(4 * 4096 * 2)
            o_num = alloc(4 * TPC * 4)
            o_den = alloc(4 * TPC * 4)
            numacc = view(o_num, 4 * TPC * 4, F32)
            denacc = view(o_den, 4 * TPC * 4, F32)
            R_acc = P.res("acc")
            R_q, R_k, R_v = P.res("Qt"), P.res("Kt"), P.res("Vt")
            cand = []
            Rcand = []
            for i in range(2):
                o = alloc(4 * 16 * 64 * 2)
                cand.append(view(o, 4 * 16 * 64 * 2))
                Rcand.append(P.res("cand%d" % i))
            Pt = []
            RPt = []
            for i in range(2):
                o = alloc(1024 * 2)
                Pt.append(view(o, 1024 * 2))
                RPt.append(P.res("Pt%d" % i))
            Vtok = []
            RVt = []
            for i in range(4):
                o = alloc(512 * 2)
                Vtok.append(view(o, 512 * 2).rearrange("p (h c) -> p h c", h=4))
                RVt.append(P.res("Vtok%d" % i))
            o = alloc(16 * 128 * 2)
            dg = view(o, 16 * 128 * 2).rearrange("p (a c) -> p a c", a=16)
            R_dg = P.res("diag")
            for side in range(2):
                for s in range(NCORE):
                    fcol = (C_FPREV if side == 0 else C_FNEXT) + s
                    P.op("dve", [R_const], [R_dg], lambda h, side=side, s=s, fcol=fcol: h.tensor_scalar(out=dg[:, side * 8 + s, :], in0=ident, scalar1=cf[:, fcol:fcol + 1], scalar2=None, op0=ALU.mult))
            gatf = gat.rearrange("r c -> (r c)")
            cstate = {"ci": 0}

            def select(side, ncols, src_fn, dst_fn, dst_res):
                nch = (ncols + 511) // 512
                for s in range(NCORE):
                    cr, cap = Rcand[cstate["ci"] % 2], cand[cstate["ci"] % 2]
                    cstate["ci"] += 1
                    P.dma("sp", [], [cr], cr, cap[:, 0:ncols], src_fn(s))
                    for c in range(nch):
                        n = min(512, ncols - c * 512)
                        P.op("pe", [cr, R_dg], [PB[c]], lambda h, c=c, n=n, s=s, cap=cap: h.matmul(
                            ps[:, c * 512:c * 512 + n], lhsT=dg[:, side * 8 + s, :], rhs=cap[:, c * 512:c * 512 + n], start=(s == 0), stop=(s == NCORE - 1)))
                for c in range(nch):
                    n = min(512, ncols - c * 512)
                    if c % 2 == 0:
                        P.op("act", [PB[c]], [dst_res], lambda h, c=c, n=n: h.activation(out=dst_fn(c * 512, n), in_=dst_shape(ps[:, c * 512:c * 512 + n]), func=AF.Copy))
                    else:
                        P.op("dve", [PB[c]], [dst_res], lambda h, c=c, n=n: h.tensor_copy(out=dst_fn(c * 512, n), in_=dst_shape(ps[:, c * 512:c * 512 + n])))
                pbi["i"] = 0

            dst_shape = lambda a: a
            for side in range(2):
                select(side, 1024,
                       lambda s, side=side: gatf[s * PKG_N + PKG_U + (1 - side) * 131072: s * PKG_N + PKG_U + (1 - side) * 131072 + 131072].rearrange("(p c) -> p c", p=128),
                       lambda c0, n, side=side: uhalo[:, side, c0:c0 + n], R_uh)
            pti = 0
            unit_ctr = {"n": 0}
            vctr = {"n": 0}
            for g in range(3):
                d = GD[g]
                Lh = LH[g]
                Lq = TPC // d
                nq = Lq // 128
                Qt = view(o_q, 4 * d * Lq * 2).rearrange("p (h r l) -> p h r l", h=4, r=d)
                Kt = view(o_k, 4 * d * Lh * 2).rearrange("p (h r l) -> p h r l", h=4, r=d)
                Vt = view(o_v, 4 * d * Lh * 2).rearrange("p (h r l) -> p h r l", h=4, r=d)
                P.dma("sp", [], [R_q], R_q, view(o_q, 4 * d * Lq * 2), qkvs[0][g])
                P.dma("sp", [], [R_k], R_k, view(o_k, 4 * d * Lh * 2), qkvs[1][g])
                P.dma("sp", [], [R_v], R_v, view(o_v, 4 * d * Lh * 2), qkvs[2][g])
                dst_shape = lambda a: a.rearrange("p (a l) -> p a l", l=64)
                for side in range(2):
                    for t in range(2):
                        Xt, Rx = (Kt, R_k) if t == 0 else (Vt, R_v)
                        l0 = 0 if side == 0 else Lh - 64
                        dsth = Xt[:, :, :, l0:l0 + 64].rearrange("p h r l -> p (h r) l")
                        ht = 1 - side
                        o0 = ((ht * 2 + t) * 21 + PKG_RR0[g]) * PKG_G
                        select(side, 4 * d * 64,
                               lambda s, o0=o0, d=d: gatf[s * PKG_N + o0: s * PKG_N + o0 + d * PKG_G].rearrange("(p c) -> p c", p=128),
                               lambda c0, n, dsth=dsth: dsth[:, c0 // 64:(c0 + n) // 64, :], Rx)
                na4 = numacc.rearrange("p (h l r) -> p h l r", h=4, r=d)
                da4 = denacc.rearrange("p (h l r) -> p h l r", h=4, r=d)

                def stageA(r, jq, vt_of):
                    def get_vtok(kt):
                        if kt in vt_of:
                            return vt_of[kt]
                        i = vctr["n"] % 4
                        vctr["n"] += 1
                        br, bap = bank()
                        b3 = bap.rearrange("p (h c) -> p h c", h=4)

                        def fn(h):
                            ins = None
                            for hh in range(4):
                                ins = h.matmul(b3[:, hh, :], lhsT=Vt[:, hh, r, kt * 128:(kt + 1) * 128], rhs=ident, start=True, stop=True)
                            return ins
                        P.op("pe", [R_v, R_const], [br], fn)
                        P.op("act", [br], [RVt[i]], lambda h: h.activation(out=Vtok[i], in_=b3, func=AF.Copy))
                        vt_of[kt] = i
                        return i
                    unit_ctr["n"] += 1
                    va = get_vtok(jq)
                    vb = get_vtok(jq + 1)
                    brs, bap = bank2()
                    b4 = bap.rearrange("p (h a q) -> p h a q", h=4, a=2)

                    def fnS(h):
                        ins = None
                        for hh in range(4):
                            for ab in range(2):
                                ins = h.matmul(b4[:, hh, ab, :], lhsT=Kt[:, hh, r, (jq + ab) * 128:(jq + ab + 1) * 128], rhs=Qt[:, hh, r, jq * 128:(jq + 1) * 128], start=True, stop=True)
                        return ins
                    P.op("pe", [R_k, R_q], brs, fnS)
                    k_ = unit_ctr["n"] % 2
                    pr, pap = RPt[k_], Pt[k_]
                    P.op("act", brs, [pr], lambda h: h.activation(out=pap, in_=bap, func=AF.Exp, scale=float(128 ** -0.5)))
                    P.op("pool", [R_const], [pr], lambda h: h.tensor_tensor(out=pap, in0=pap, in1=mask4, op=ALU.mult))
                    return dict(r=r, jq=jq, va=va, vb=vb, pr=pr, pap=pap)

                def stageB(cx):
                    r, jq, va, vb, pr, pap = cx["r"], cx["jq"], cx["va"], cx["vb"], cx["pr"], cx["pap"]
                    p4 = pap.rearrange("p (h a q) -> p h a q", h=4, a=2)
                    nr, nap = bank()
                    n3 = nap.rearrange("p (h q) -> p h q", h=4)

                    def fnN(h):
                        ins = None
                        for hh in range(4):
                            ins = h.matmul(n3[:, hh, :], lhsT=Vtok[va][:, hh, :], rhs=p4[:, hh, 0, :], start=True, stop=False)
                            ins = h.matmul(n3[:, hh, :], lhsT=Vtok[vb][:, hh, :], rhs=p4[:, hh, 1, :], start=False, stop=True)
                        return ins
                    P.op("pe", [pr, RVt[va], RVt[vb]], [nr], fnN)
                    dr_, dap = bank()
                    d3 = dap.rearrange("p (h q) -> p h q", h=4)
                    oa = ones_f if jq == 0 else ones
                    ob = ones_l if jq + 1 == nq else ones

                    def fnD(h):
                        h.matmul(d3, lhsT=oa, rhs=p4[:, :, 0, :], start=True, stop=False)
                        return h.matmul(d3, lhsT=ob, rhs=p4[:, :, 1, :], start=False, stop=True)
                    P.op("pe", [pr, R_const], [dr_], fnD)
                    nsl = na4[:, :, jq * 128:(jq + 1) * 128, r]
                    dsl = da4[:, :, jq * 128:(jq + 1) * 128, r]
                    if g == 0:
                        P.op("dve", [nr, R_acc], [], lambda h: h.tensor_copy(out=nsl, in_=n3))
                        P.op("act", [dr_, R_acc], [], lambda h: h.activation(out=dsl, in_=d3, func=AF.Copy))
                    else:
                        P.op("dve", [nr, R_acc], [], lambda h: h.tensor_tensor(out=nsl, in0=n3, in1=nsl, op=ALU.add))
                        P.op("dve", [dr_, R_acc], [], lambda h: h.tensor_tensor(out=dsl, in0=d3, in1=dsl, op=ALU.add))

                pend = None
                for r in range(d):
                    vt_of = {}
                    for jq in range(nq):
                        cx = stageA(r, jq, vt_of)
                        if pend is not None:
                            stageB(pend)
                        pend = cx
                stageB(pend)
                P.op("dve", [R_q, R_k, R_v], [R_acc], lambda h: h.memset(cand[0][:, 0:2], 0.0))
                P.op("dve", [], [R_q, R_k, R_v, Rcand[0]], lambda h: h.memset(cand[0][:, 0:2], 0.0))
            aost = view(o_k, 4 * TPC * 2)
            P.op("dve", [], [R_acc], lambda h: h.reciprocal(out=denacc, in_=denacc))
            P.op("dve", [], [R_acc, R_k], lambda h: h.tensor_tensor(out=aost, in0=numacc, in1=denacc, op=ALU.mult))
            P.dma("sp", [R_k], [], R_k, aos, aost)
            P.barrier()

        if STOP_AFTER >= 4:
            state["off"] = phase_mark
            setup_wslots()
            cm = setup_common()
            xs, XS, cp, CP = cm["xs"], cm["XS"], cm["cp"], cm["CP"]
            HID0 = 16
            QM0, PL0, MX0, MG0, OM0 = 16, 24, 32, 40, 56
            UT0 = 40
            gt = []
            Rgt = []
            for i in range(3):
                o = alloc(T * 4)
                gt.append(view(o, T * 4, F32))
                Rgt.append(P.res("gt%d" % i))
            o = alloc(4 * T * 2)
            aoT = view(o, 4 * T * 2).rearrange("p (h t) -> p h t", h=4)
            R_ao = P.res("aoT")
            o = alloc(2 * T * 2)
            Pm = view(o, 2 * T * 2).rearrange("p (a t) -> p a t", a=2)
            R_pm = P.res("Pm")
            R_out = P.res("outst")

            sched3 = [B_MEMKV + i for i in range(4)]
            for m in range(NT):
                sched3 += [B_QMEM, B_QMEM + 1, B_POOLW]
                sched3 += [B_GU3 + i for i in range(32)]
                sched3 += [B_WOUT + i for i in range(4)]
                sched3 += [B_FFN2 + i for i in range(38)]
            conv_upto(B_FFN2 + 1)
            wstart(sched3)
            wsl["pace"] = 3
            memT3 = memTd.rearrange("(k p) t -> p k t", p=128)
            P.dma("sp", [], XS, XS[0], xs[:, :, 0:256], memT3)
            rmsnorm(cm, C_GAIN + 64, lambda k: (CP[k], cp[:, k, 0:256]), ncols=256)
            for j in range(2):
                wr, wap = wnext(B_MEMKV + j)
                w3 = wap.rearrange("p (k c) -> p k c", k=KC)
                for q4 in range(4):
                    br, bap = bank()
                    mm(br, bap[:, 0:256], [(w3[:, k, q4 * 128:(q4 + 1) * 128], cp[:, k, 0:256]) for k in range(KC)], [wr] + CP[0:KC])
                    P.op("act", [br], [R_km], lambda h, bap=bap, j=j, q4=q4: h.activation(out=kmT[:, j * 4 + q4, :], in_=bap[:, 0:256], func=AF.Copy))
            for j in range(2):
                wr, wap = wnext(B_MEMKV + 2 + j)
                w3 = wap.rearrange("p (k c) -> p k c", k=KC)
                for mt in range(2):
                    br, bap = bank()
                    mm(br, bap, [(cp[:, k, mt * 128:(mt + 1) * 128], w3[:, k, :]) for k in range(KC)], [wr] + CP[0:KC])
                    P.op("act", [br], [R_km], lambda h, bap=bap, j=j, mt=mt: h.activation(out=vm[:, mt, j * 512:(j + 1) * 512], in_=bap, func=AF.Copy))
            bands = cb[:, CB_BAND:CB_BAND + 4608].rearrange("p (w v n t) -> p w v n t", w=4, v=3, n=3)
            icnt = cf[:, C_ICNT:C_ICNT + 1536].rearrange("p (w v t) -> p w v t", w=4, v=3)
            for m in range(NT):
                tsl = slice(m * T, (m + 1) * T)
                for k in range(KC):
                    P.dma("sp", [], [XS[k]], XS[k], xs[:, k, :], x1s3[:, k, tsl])
                P.dma("sp", [], [R_ao], R_ao, aoT, aos.rearrange("p (h t) -> p h t", h=4)[:, :, tsl])
                lo = max(0, 4 * m - 1)
                hi = min(15, 4 * m + 4)
                s0 = lo - (4 * m - 1)
                nt_ = hi - lo + 1
                ut = cp[:, UT0:UT0 + 12, :].rearrange("p (a b) t -> p a (b t)", a=6)
                P.dma("sp", [], CP[UT0 + 2 * s0:UT0 + 2 * (s0 + nt_)], CP[UT0], ut[:, s0:s0 + nt_, :], us[lo * 128:(hi + 1) * 128, :].rearrange("(a p) c -> p a c", p=128))
                rmsnorm(cm, C_GAIN + 16, lambda k: (CP[k], cp[:, k, :]))
                for j in range(2):
                    wr, wap = wnext(B_QMEM + j)
                    w3 = wap.rearrange("p (k c) -> p k c", k=KC)
                    for q4 in range(4):
                        br, bap = bank()
                        mm(br, bap, [(w3[:, k, q4 * 128:(q4 + 1) * 128], cp[:, k, :]) for k in range(KC)], [wr] + CP[0:KC])
                        c = QM0 + j * 4 + q4
                        P.op("act", [br], [CP[c]], lambda h, bap=bap, c=c: h.activation(out=cp[:, c, :], in_=bap, func=AF.Copy))
                for hm in range(4):
                    brs, bap = bank2()
                    b3 = bap.rearrange("p (a t) -> p a t", a=2)

                    def fnS(h, hm=hm, b3=b3):
                        ins = None
                        for mt in range(2):
                            for c2 in range(2):
                                ins = h.matmul(b3[:, mt, :], lhsT=kmT[:, hm * 2 + c2, mt * 128:(mt + 1) * 128], rhs=cp[:, QM0 + hm * 2 + c2, :], start=(c2 == 0), stop=(c2 == 1))
                        return ins
                    P.op("pe", [R_km, CP[QM0 + hm * 2], CP[QM0 + hm * 2 + 1]], brs, fnS)
                    P.op("act", brs, [R_pm], lambda h, b3=b3: h.activation(out=Pm, in_=b3, func=AF.Exp, scale=float(256 ** -0.5)))
                    dr_, dap = bank()
                    mm(dr_, dap, [(ones, Pm[:, mt, :]) for mt in range(2)], [R_pm, R_const])
                    rr, rap = tmpf(cm)
                    P.op("dve", [dr_], [rr], lambda h, dap=dap, rap=rap: h.reciprocal(out=rap, in_=dap))
                    for c2 in range(2):
                        orr, oap = bank()
                        mm(orr, oap, [(vm[:, mt, hm * 256 + c2 * 128:hm * 256 + (c2 + 1) * 128], Pm[:, mt, :]) for mt in range(2)], [R_pm, R_km])
                        c = OM0 + hm * 2 + c2
                        P.op("dve", [orr, rr], [CP[c]], lambda h, oap=oap, rap=rap, c=c: h.tensor_tensor(out=cp[:, c, :], in0=oap, in1=rap, op=ALU.mult))
                for tt in range(4):
                    i_t = 4 * m + tt
                    var = 0 if i_t == 0 else (2 if i_t == 15 else 1)
                    brs, bap = bank2()
                    b4 = bap.rearrange("p (w c t) -> p w c t", w=4, c=2)

                    def usrc(nb, ch, i_t=i_t, m=m):
                        ti = i_t + nb - 1
                        if ti < 0:
                            return uhalo[:, 0, ch * 128:(ch + 1) * 128], R_uh
                        if ti > 15:
                            return uhalo[:, 1, ch * 128:(ch + 1) * 128], R_uh
                        sl_ = ti - (4 * m - 1)
                        return ut[:, sl_, ch * 128:(ch + 1) * 128], CP[UT0 + 2 * sl_ + (ch // 4)]
                    rd = set()
                    plist = []
                    for wg in range(4):
                        for c2 in range(2):
                            ch = wg * 2 + c2
                            for nb in range(3):
                                ap_, r_ = usrc(nb, ch)
                                rd.add(r_)
                                plist.append((b4[:, wg, c2, :], ap_, bands[:, wg, var, nb, :], nb))

                    def fnP(h, plist=plist):
                        ins = None
                        for (o_, l_, r_, nb) in plist:
                            ins = h.matmul(o_, lhsT=l_, rhs=r_, start=(nb == 0), stop=(nb == 2))
                        return ins
                    P.op("pe", list(rd) + [R_const], brs, fnP)
                    pl = cp[:, PL0:PL0 + 8, tt * 128:(tt + 1) * 128].rearrange("p (w c) t -> p w c t", w=4)
                    for c2 in range(2):
                        P.op("dve", brs + [R_const], CP[PL0:PL0 + 8], lambda h, b4=b4, pl=pl, var=var, c2=c2: h.tensor_tensor(out=pl[:, :, c2, :], in0=b4[:, :, c2, :], in1=icnt[:, :, var, :], op=ALU.mult))
                wr, wap = wnext(B_POOLW)
                pw4 = wap[:, 0:2048].rearrange("p (g k c) -> p g k c", g=4, k=2)
                for wg in range(4):
                    for dc in range(2):
                        br, bap = bank()
                        mm(br, bap, [(pw4[:, wg, k2, dc * 128:(dc + 1) * 128], cp[:, PL0 + wg * 2 + k2, :]) for k2 in range(2)], [wr] + CP[PL0 + wg * 2:PL0 + wg * 2 + 2])
                        c = wg * 2 + dc
                        P.op("act", [br, R_const], [CP[MX0 + c]], lambda h, bap=bap, c=c: h.activation(out=cp[:, MX0 + c, :], in_=bap, func=AF.Identity, scale=cf[:, C_PS + c:C_PS + c + 1]))
                for n2 in range(8):
                    for sub in range(2):
                        c = n2 * 2 + sub
                        wr, wap = wnext(B_GU3 + c * 2)
                        w3 = wap[:, 0:KC * 384].rearrange("p (k c) -> p k c", k=KC)
                        for i in range(3):
                            br, bap = bank()
                            mm(br, bap, [(w3[:, k, i * 128:(i + 1) * 128], cp[:, k, :]) for k in range(KC)], [wr] + CP[0:KC])
                            P.op("act", [br, R_const], [Rgt[i]], lambda h, bap=bap, i=i, c=c: h.activation(out=gt[i], in_=bap, func=AF.Sigmoid, bias=cf[:, C_BG + i * 16 + c:C_BG + i * 16 + c + 1]))
                        ur_, uap_ = wnext(B_GU3 + c * 2 + 1)
                        u3 = uap_[:, 0:20 * 128].rearrange("p (k c) -> p k c", k=20)
                        ucol = slice(0, 128)
                        y0r, y0 = bank()
                        mm(y0r, y0, [(u3[:, k, ucol], cp[:, MX0 + k, :]) for k in range(8)], [ur_] + CP[MX0:MX0 + 8])
                        t0r, t0 = tmpf(cm)
                        P.op("dve", [y0r, Rgt[0]], [t0r], lambda h, y0=y0, t0=t0: h.tensor_tensor(out=t0, in0=y0, in1=gt[0], op=ALU.mult))
                        y1r, y1 = bank()
                        mm(y1r, y1, [(u3[:, 8 + k, ucol], aoT[:, k, :]) for k in range(4)], [ur_, R_ao])
                        t1r, t1 = tmpf(cm)
                        P.op("dve", [y1r, Rgt[1]], [t1r], lambda h, y1=y1, t1=t1: h.tensor_tensor(out=t1, in0=y1, in1=gt[1], op=ALU.mult))
                        P.op("pool", [t1r], [t0r], lambda h, t0=t0, t1=t1: h.tensor_tensor(out=t0, in0=t0, in1=t1, op=ALU.add))
                        y2r, y2 = bank()
                        mm(y2r, y2, [(u3[:, 12 + k, ucol], cp[:, OM0 + k, :]) for k in range(8)], [ur_] + CP[OM0:OM0 + 8])
                        t2r, t2 = tmpf(cm)
                        P.op("dve", [y2r, Rgt[2]], [t2r], lambda h, y2=y2, t2=t2: h.tensor_tensor(out=t2, in0=y2, in1=gt[2], op=ALU.mult))
                        P.op("pool", [t0r, t2r], [CP[MG0 + c]], lambda h, t0=t0, t2=t2, c=c: h.tensor_tensor(out=cp[:, MG0 + c, :], in0=t0, in1=t2, op=ALU.add))
                for j in range(4):
                    wr, wap = wnext(B_WOUT + j)
                    w3 = wap.rearrange("p (k c) -> p k c", k=KC)
                    for q4 in range(4):
                        n = j * 4 + q4
                        br, bap = bank()
                        mm(br, bap, [(w3[:, k, q4 * 128:(q4 + 1) * 128], cp[:, MG0 + k, :]) for k in range(KC)], [wr] + CP[MG0:MG0 + KC])
                        P.op("dve", [br], [XS[n]], lambda h, bap=bap, n=n: h.tensor_tensor(out=xs[:, n, :], in0=bap, in1=xs[:, n, :], op=ALU.add))
                        stats_chunk(cm, n)
                norm_finish(cm, C_GAIN + 32, lambda k: (CP[k], cp[:, k, :]))
                ffn(cm, B_FFN2, HID0)
                norm_finish(cm, C_GAIN + 48, lambda k: (XS[k], xs[:, k, :]),
                            after_k=lambda k: P.dma("pool", [XS[k]], [], R_out, outT3[:, k, tsl], xs[:, k, :]))
        P.barrier()
    return nc


_CACHE = {}


def kernel(**inp):
    inp = {k: np.asarray(v) for k, v in inp.items()}
    wallh = build_wall(inp)
    x = inp["x"][0]
    memT = np.ascontiguousarray(inp["mem"][0].T.astype(np.float32))
    pos = inp["positions"][0].astype(np.int32)
    in_maps = []
    for c in range(NCORE):
        cfh, cbh = build_consts(inp, c)
        xTc = np.ascontiguousarray(x[c * TPC:(c + 1) * TPC, :].T)
        posc = np.ascontiguousarray(np.broadcast_to(pos[c * TPC:(c + 1) * TPC][None, :], (128, TPC)))
        in_maps.append({"xT": xTc, "wall": wallh, "cf": cfh, "cb": cbh, "pos": posc, "memT": memT})
    if "nc" not in _CACHE:
        _CACHE["nc"] = build_program()
    res = run_bass_kernel_spmd(_CACHE["nc"], in_maps, core_ids=list(range(NCORE)))
    _CACHE["res"] = res
    out = np.empty((1, S, D), np.float32)
    for c in range(NCORE):
        out[0, c * TPC:(c + 1) * TPC, :] = res.results[c]["outT"].T
    return out
```

```python
import os
from contextlib import ExitStack
import numpy as np
import concourse.bass as bass
import concourse.mybir as mybir
from concourse.bass_utils import run_bass_kernel_spmd

F32 = mybir.dt.float32
BF16 = mybir.dt.bfloat16
I32 = mybir.dt.int32
ALU = mybir.AluOpType
AF = mybir.ActivationFunctionType

NCORE = 8
S = 16384
D = 2048
TPC = S // NCORE
T = 512
NT = TPC // T
KC = D // 128
DFF = 5632
FC = DFF // 128
HF = FC // 2
BLK = 8192
GD = (1, 4, 16)
EPS = 1e-6
DEBUG = bool(int(os.environ.get("KDEBUG", "0")))
STOP_AFTER = int(os.environ.get("KSTOP", "9"))

B_FFN1 = 0
B_WIN1 = 38
B_MEMKV = 49
B_QMEM = 53
B_POOLW = 55
B_GU3 = 56
B_WOUT = 88
B_FFN2 = 92
NBLK = 130


def blk_used(b):
    if B_FFN1 <= b < B_FFN1 + 38 or B_FFN2 <= b < B_FFN2 + 38:
        r = (b - (B_FFN1 if b < 38 else B_FFN2)) % 19
        return 8192 if r < 11 else 22 * 256
    if b == B_POOLW:
        return 2048
    if B_GU3 <= b < B_GU3 + 32:
        return 16 * 384 if (b - B_GU3) % 2 == 0 else 20 * 128
    return 8192

C_GAIN = 0
C_BG = 80
C_PS = 128
C_INVF = 136
C_SSIGN = 137
C_FPREV = 138
C_FNEXT = 146
C_VPREV = 154
C_VNEXT = 155
C_EPS = 156
C_ICNT = 160
NCF = 160 + 1536
CB_ID = 0
CB_MA = 128
CB_MB = 256
CB_BAND = 384
NCB = 384 + 4 * 3 * 3 * 128

PKG_G = 128 * 4 * 64
PKG_RR0 = (0, 1, 5)
PKG_U = 2 * 21 * PKG_G
PKG_N = PKG_U + 128 * 1024
PKG_ROWS = PKG_N // 256


def _ffn_blocks(wg, wu, wd):
    out = []
    for j in range(22):
        b = np.concatenate([wg[:, 256 * j:256 * j + 256], wu[:, 256 * j:256 * j + 256]], axis=1)
        out.append(b.reshape(16, 128, 512).transpose(1, 0, 2).reshape(128, 8192))
    for half in range(2):
        for n2 in range(8):
            b = wd[half * 2816:(half + 1) * 2816, n2 * 256:(n2 + 1) * 256]
            bb = np.zeros((128, BLK), np.float32)
            bb[:, :22 * 256] = b.reshape(22, 128, 256).transpose(1, 0, 2).reshape(128, 22 * 256)
            out.append(bb)
    res = []
    for half in range(2):
        res += out[half * 11:(half + 1) * 11] + out[22 + half * 8:22 + (half + 1) * 8]
    return res


def _kblock(w):
    K, C = w.shape
    kc = K // 128
    bb = np.zeros((128, BLK), np.float32)
    bb[:, :kc * C] = w.reshape(kc, 128, C).transpose(1, 0, 2).reshape(128, kc * C)
    return bb


def build_wall(inp):
    f1 = _ffn_blocks(inp["ffn1_w_gate"][0], inp["ffn1_w_up"][0], inp["ffn1_w_down"][0])
    f2 = _ffn_blocks(inp["ffn2_w_gate"][0], inp["ffn2_w_up"][0], inp["ffn2_w_down"][0])
    win = inp["w_in"][0]
    blocks = []
    blocks += f1
    blocks.append(_kblock(win[:, 0:512]))
    blocks.append(_kblock(win[:, 512:1024]))
    for g in range(3):
        for t in range(3):
            c0 = 1024 + t * 1536 + g * 512
            blocks.append(_kblock(win[:, c0:c0 + 512]))
    wkv = inp["w_mem_kv"][0]
    for j in range(4):
        blocks.append(_kblock(wkv[:, j * 512:(j + 1) * 512]))
    blocks.append(_kblock(win[:, 5632:6144]))
    blocks.append(_kblock(win[:, 6144:6656]))
    pw = inp["pool_w"][0]
    bb = np.zeros((128, BLK), np.float32)
    bb[:, :8 * 256] = pw.reshape(4, 2, 128, 256).transpose(2, 0, 1, 3).reshape(128, 8 * 256)
    blocks.append(bb)
    wup = np.concatenate([inp["w_up_pool"][0], inp["w_up_attn"][0], inp["w_up_mem"][0]], axis=0)
    for c in range(16):
        cols = np.concatenate([win[:, 6656 + i * 2048 + c * 128: 6656 + i * 2048 + (c + 1) * 128] for i in range(3)], axis=1)
        blocks.append(_kblock(cols))
        blocks.append(_kblock(wup[:, c * 128:(c + 1) * 128]))
    wo = inp["w_out"][0]
    for j in range(4):
        blocks.append(_kblock(wo[:, j * 512:(j + 1) * 512]))
    blocks += f2
    assert len(blocks) == NBLK
    return np.ascontiguousarray(np.stack(blocks, 0).reshape(NBLK * 128, BLK))


def build_consts(inp, c):
    cf = np.zeros((128, NCF), np.float32)
    for i, nm in enumerate(["ffn1_norm", "mix_norm", "ffn2_norm", "final_norm", "mem_norm"]):
        v = np.asarray(inp[nm], np.float32).reshape(-1)
        cf[:, C_GAIN + 16 * i:C_GAIN + 16 * (i + 1)] = v.reshape(16, 128).T
    cf[:, C_BG:C_BG + 48] = np.asarray(inp["b_gate"], np.float32).reshape(48, 128).T
    cf[:, C_PS:C_PS + 8] = np.asarray(inp["pool_scale"], np.float32).reshape(8, 128).T
    half = 64
    invf = (10000.0 ** (-np.arange(half, dtype=np.float32) / half)).astype(np.float32)
    cf[:, C_INVF] = np.concatenate([invf, invf]) / np.float32(2 * np.pi)
    cf[:, C_EPS] = EPS
    cf[:64, C_SSIGN] = -1.0
    cf[64:, C_SSIGN] = 1.0
    if c > 0:
        cf[:, C_FPREV + c - 1] = 1.0
        cf[:, C_VPREV] = 1.0
    if c < NCORE - 1:
        cf[:, C_FNEXT + c + 1] = 1.0
        cf[:, C_VNEXT] = 1.0
    cb = np.zeros((128, NCB), np.float32)
    cb[:, CB_ID:CB_ID + 128] = np.eye(128, dtype=np.float32)
    i = np.arange(128)[:, None]
    j = np.arange(128)[None, :]
    cb[:, CB_MA:CB_MA + 128] = (i >= j)
    cb[:, CB_MB:CB_MB + 128] = (i <= j)
    for wg, w in enumerate((2, 4, 8, 16)):
        hw = w // 2
        for var, ti in enumerate((0, 1, 15)):
            g0 = c * TPC + ti * 128
            P = g0 + np.arange(128)
            lo = np.clip(P - hw, 0, S)
            hi = np.clip(P + hw + 1, 0, S)
            cnt = (hi - lo).astype(np.float32)
            cf[:, C_ICNT + (wg * 3 + var) * 128:C_ICNT + (wg * 3 + var + 1) * 128] = (1.0 / cnt)[None, :]
            for nb in range(3):
                Pp = g0 + (nb - 1) * 128 + np.arange(128)
                B = ((Pp[:, None] >= lo[None, :]) & (Pp[:, None] < hi[None, :])).astype(np.float32)
                B -= (Pp[:, None] == P[None, :]) * cnt[None, :]
                o = CB_BAND + ((wg * 3 + var) * 3 + nb) * 128
                cb[:, o:o + 128] = B
    return cf, cb


class Res:
    __slots__ = ("name", "w", "r", "dsem", "dkey", "dcount")

    def __init__(self, name):
        self.name = name
        self.w = []
        self.r = {}
        self.dsem = None
        self.dkey = None
        self.dcount = 0


class Prog:
    def __init__(self, nc, stack):
        self.nc = nc
        self.stack = stack
        self.sems = {}
        self.eng = {}
        self.allres = []
        for key, h in (("pe", nc.tensor), ("dve", nc.vector), ("act", nc.scalar), ("pool", nc.gpsimd), ("sp", nc.sync)):
            sem = stack.enter_context(nc.semaphore("e_" + key))
            self.sems[key] = sem
            self.eng[key] = {"h": h, "count": 0, "seen": {}}
        self.nsem = 0

    def res(self, name):
        r = Res(name)
        self.allres.append(r)
        return r

    def _dsem(self, r):
        if r.dsem is None:
            self.nsem += 1
            r.dkey = "d%d_%s" % (self.nsem, r.name)
            r.dsem = self.stack.enter_context(self.nc.semaphore(r.dkey))
            self.sems[r.dkey] = r.dsem
        return r.dsem

    def _wait(self, ek, toks):
        e = self.eng[ek]
        need = {}
        for (k, v) in toks:
            if k == ek and ek in ("pe",):
                continue
            if e["seen"].get(k, 0) < v:
                need[k] = max(need.get(k, 0), v)
        for k, v in need.items():
            e["h"].wait_ge(self.sems[k], v)
            e["seen"][k] = v

    @staticmethod
    def _deps(reads, writes):
        toks = []
        for r in reads:
            toks += r.w
        for w in writes:
            toks += w.w
            toks += list(w.r.items())
        return toks

    @staticmethod
    def _commit(reads, writes, tok):
        for r in reads:
            if r.r.get(tok[0], 0) < tok[1]:
                r.r[tok[0]] = tok[1]
        for w in writes:
            w.w = [tok]
            w.r = {}

    def op(self, ek, reads, writes, fn):
        self._wait(ek, self._deps(reads, writes))
        e = self.eng[ek]
        ins = fn(e["h"])
        e["count"] += 1
        ins.then_inc(self.sems[ek], 1)
        self._commit(reads, writes, (ek, e["count"]))

    def dma(self, qk, reads, writes, semres, out_ap, in_ap):
        self._wait(qk, self._deps(reads, writes))
        sem = self._dsem(semres)
        ins = self.eng[qk]["h"].dma_start(out=out_ap, in_=in_ap)
        semres.dcount += 16
        ins.then_inc(sem, 16)
        self._commit(reads, writes, (semres.dkey, semres.dcount))

    def barrier(self):
        toks = [(k, e["count"]) for k, e in self.eng.items() if e["count"] > 0 and k != "sp"]
        for r in self.allres:
            if r.dsem is not None and r.dcount > 0:
                toks.append((r.dkey, r.dcount))
        for ek in self.eng:
            e = self.eng[ek]
            for (k, v) in toks:
                if k == ek:
                    continue
                if e["seen"].get(k, 0) < v:
                    e["h"].wait_ge(self.sems[k], v)
                    e["seen"][k] = v
        for r in self.allres:
            r.w = [t for t in r.w if t[0] in ("dve", "act", "pool")]
            r.r = {k: v for k, v in r.r.items() if k in ("dve", "act", "pool")}


def build_program():
    nc = bass.Bass("TRN2", target_bir_lowering=False)
    dk = "ExternalOutput" if DEBUG else "Internal"
    xT = nc.dram_tensor("xT", [D, TPC], F32, kind="ExternalInput").ap()
    wall = nc.dram_tensor("wall", [NBLK * 128, BLK], F32, kind="ExternalInput").ap()
    cfd = nc.dram_tensor("cf", [128, NCF], F32, kind="ExternalInput").ap()
    cbd = nc.dram_tensor("cb", [128, NCB], F32, kind="ExternalInput").ap()
    posd = nc.dram_tensor("pos", [128, TPC], I32, kind="ExternalInput").ap()
    memTd = nc.dram_tensor("memT", [D, 256], F32, kind="ExternalInput").ap()
    outT = nc.dram_tensor("outT", [D, TPC], F32, kind="ExternalOutput").ap()
    WSPLIT = 64
    wbf_a = nc.dram_tensor("wbf_a", [WSPLIT * 128, BLK], BF16, kind="Internal").ap()
    wbf_b = nc.dram_tensor("wbf_b", [(NBLK - WSPLIT) * 128, BLK], BF16, kind="Internal").ap()

    def wbf_rows(b0, b1):
        if b1 <= WSPLIT:
            return wbf_a[b0 * 128:b1 * 128, :]
        assert b0 >= WSPLIT
        return wbf_b[(b0 - WSPLIT) * 128:(b1 - WSPLIT) * 128, :]
    x1s = nc.dram_tensor("x1s", [D, TPC], F32, kind=dk).ap()
    us = nc.dram_tensor("us", [16 * 128, 1024], BF16, kind=dk).ap()
    aos = nc.dram_tensor("aos", [128, 4 * TPC], BF16, kind=dk).ap()
    LH = [TPC // d + 128 for d in GD]
    LQ = [TPC // d for d in GD]
    qkvs = [[nc.dram_tensor("qkv%d_%d" % (t, g), [128, 4 * GD[g] * (LQ[g] if t == 0 else LH[g])], BF16, kind=dk).ap() for g in range(3)] for t in range(3)]
    pkgs = [nc.dram_tensor("pkg%d" % i, [PKG_ROWS, 256], BF16, kind="Internal").ap() for i in range(2)]
    gats = [nc.dram_tensor("gat%d" % i, [NCORE * PKG_ROWS, 256], BF16, kind="Internal").ap() for i in range(2)]

    with ExitStack() as stack:
        arena = stack.enter_context(nc.sbuf_tensor("arena", [128, 103 * 1024 + 384], BF16))
        ps = stack.enter_context(nc.psum_tensor("ps", [128, 4096], F32))
        P = Prog(nc, stack)
        cc_sem = stack.enter_context(nc.semaphore("cc_sem"))
        P.sems["cc"] = cc_sem

        state = {"off": 0}

        def alloc(nbytes):
            o = state["off"]
            state["off"] = o + ((nbytes + 63) // 64) * 64
            assert state["off"] <= (103 * 1024 + 384) * 2, state["off"]
            return o

        def view(off, nbytes, dt=BF16):
            a = arena[:, off // 2:(off + nbytes) // 2]
            if dt is not BF16:
                a = a.bitcast(dt)
            return a

        o_cf = alloc(NCF * 4)
        cf = view(o_cf, NCF * 4, F32)
        o_cb = alloc(NCB * 2)
        cb = view(o_cb, NCB * 2)
        o_misc = alloc(128 * 2 * 3 + 2048)
        ones = view(o_misc, 256)
        ones_f = view(o_misc + 256, 256)
        ones_l = view(o_misc + 512, 256)
        mask4 = view(o_misc + 768, 2048)
        o_uh = alloc(4096)
        uhalo = view(o_uh, 4096).rearrange("p (a c) -> p a c", a=2)
        o_km = alloc(8192)
        kmT = view(o_km, 4096).rearrange("p (a c) -> p a c", a=8)
        vm = view(o_km + 4096, 4096).rearrange("p (a c) -> p a c", a=2)
        R_const = P.res("const")
        R_uh = P.res("uhalo")
        R_km = P.res("kmvm")
        phase_mark = state["off"]

        ident = cb[:, CB_ID:CB_ID + 128]

        PB = [P.res("pb%d" % i) for i in range(8)]
        pbi = {"i": 0}

        STB = 7

        def bank():
            i = pbi["i"]
            pbi["i"] = (i + 1) % 7
            return PB[i], ps[:, i * 512:(i + 1) * 512]

        def bank2():
            i = pbi["i"]
            if i % 2:
                i += 1
            if i >= 6:
                i = 0
            pbi["i"] = (i + 2) % 7
            return [PB[i], PB[i + 1]], ps[:, i * 512:(i + 2) * 512]

        def mm(bres, out_ap, pairs, reads):
            def fn(h):
                n = len(pairs)
                ins = None
                for i, (l, r) in enumerate(pairs):
                    ins = h.matmul(out_ap, lhsT=l, rhs=r, start=(i == 0), stop=(i == n - 1))
                return ins
            P.op("pe", reads, bres if isinstance(bres, list) else [bres], fn)

        conv_res = [Res("conv%d" % i) for i in range(16)]
        blk_tok = {}
        chunks = [(b, b + 1) for b in range(NBLK)]
        conv_state = {"ci": 0}

        def issue_conv():
            ci = conv_state["ci"]
            if ci >= len(chunks):
                return False
            b0, b1 = chunks[ci]
            r = conv_res[ci % 16]
            sem = P._dsem(r)
            ins = nc.gpsimd.dma_start(out=wbf_rows(b0, b1)[:, 0:blk_used(b0)], in_=wall[b0 * 128:b1 * 128, 0:blk_used(b0)])
            r.dcount += 16
            ins.then_inc(sem, 16)
            for bb in range(b0, b1):
                blk_tok[bb] = (r.dkey, r.dcount)
            conv_state["ci"] = ci + 1
            return True

        def conv_upto(blk, paced=True):
            while conv_state["ci"] < len(chunks) and chunks[conv_state["ci"]][0] <= blk:
                if paced:
                    P._wait("pool", [("pe", P.eng["pe"]["count"])])
                issue_conv()

        conv_upto(7, paced=False)

        NSLOT = 3
        wsl = {"n": 0, "sched": [], "pos": 0, "issued": 0}

        def setup_wslots():
            offs = [alloc(BLK * 2) for _ in range(NSLOT)]
            wsl["ap"] = [view(o, BLK * 2) for o in offs]
            wsl["res"] = [P.res("wslot%d" % i) for i in range(NSLOT)]

        def _issue_next():
            i = wsl["issued"]
            if i >= len(wsl["sched"]):
                return
            blk = wsl["sched"][i]
            s = i % NSLOT
            r = wsl["res"][s]
            P._wait("sp", [blk_tok[blk]])
            P.dma("sp", [], [r], r, wsl["ap"][s][:, 0:blk_used(blk)], wbf_rows(blk, blk + 1)[:, 0:blk_used(blk)])
            wsl["issued"] = i + 1

        def wstart(sched):
            wsl["sched"] = sched
            wsl["pos"] = 0
            wsl["issued"] = 0
            for _ in range(NSLOT - 1):
                _issue_next()

        def wnext(expect):
            i = wsl["pos"]
            assert wsl["sched"][i] == expect, (i, wsl["sched"][i], expect)
            while wsl["issued"] < min(i + NSLOT, len(wsl["sched"])):
                if wsl["issued"] - i >= NSLOT:
                    break
                _issue_next()
            wsl["pos"] = i + 1
            s = i % NSLOT
            if wsl.get("pace"):
                if i < 49 and wsl["pace"] == 1:
                    conv_upto(min(wsl["sched"][i] + 8, 52))
                elif wsl["pace"] == 3:
                    conv_upto(min(wsl["sched"][i] + 8, NBLK - 1))
                elif (i - 49) % 2 == 0 and conv_state["ci"] < len(chunks) and chunks[conv_state["ci"]][0] < 116:
                    P._wait("pool", [("pe", P.eng["pe"]["count"])])
                    issue_conv()
            return wsl["res"][s], wsl["ap"][s]

        def setup_common(ncp=64):
            d = {}
            o = alloc(KC * T * 4)
            d["xs"] = view(o, KC * T * 4, F32).rearrange("p (k t) -> p k t", k=KC)
            d["XS"] = [P.res("xs%d" % k) for k in range(KC)]
            o = alloc(ncp * T * 2)
            d["cp"] = view(o, ncp * T * 2).rearrange("p (k t) -> p k t", k=ncp)
            d["CP"] = [P.res("cp%d" % k) for k in range(ncp)]
            o = alloc(T * 4)
            d["rstd"] = view(o, T * 4, F32)
            d["Rrstd"] = P.res("rstd")
            d["tmp"] = []
            d["Rtmp"] = []
            for i in range(4):
                o = alloc(T * 4)
                d["tmp"].append(view(o, T * 4, F32))
                d["Rtmp"].append(P.res("tmp%d" % i))
            d["ti"] = 0
            return d

        def tmpf(cm):
            i = cm["ti"]
            cm["ti"] = (i + 1) % 4
            return cm["Rtmp"][i], cm["tmp"][i]

        def stats_sq(cm, k, ncols=T):
            xs, XS, cp, CP = cm["xs"], cm["XS"], cm["cp"], cm["CP"]
            sl = slice(0, ncols)
            P.op("act", [XS[k]], [CP[k]], lambda h: h.activation(out=cp[:, k, sl], in_=xs[:, k, sl], func=AF.Square))

        def stats_mm(cm, k, ncols=T):
            cp, CP = cm["cp"], cm["CP"]
            sl = slice(0, ncols)
            P.op("pe", [CP[k], R_const], [PB[STB]], lambda h: h.matmul(ps[:, STB * 512:STB * 512 + ncols], lhsT=ones, rhs=cp[:, k, sl], start=(k == 0), stop=(k == KC - 1)))

        def stats_chunk(cm, k, ncols=T):
            stats_sq(cm, k, ncols)
            stats_mm(cm, k, ncols)

        def stats_delayed(cm, n, lag=3):
            stats_sq(cm, n)
            if n - lag >= 0:
                stats_mm(cm, n - lag)
            if n == KC - 1:
                for k in range(max(0, KC - lag), KC):
                    stats_mm(cm, k)

        def norm_finish(cm, gcol, dst_fn, ncols=T, after_k=None):
            xs, XS = cm["xs"], cm["XS"]
            sl = slice(0, ncols)
            rstd, Rr = cm["rstd"], cm["Rrstd"]
            bap = ps[:, STB * 512:STB * 512 + ncols]
            P.op("act", [PB[STB], R_const], [Rr], lambda h: h.activation(out=rstd[:, sl], in_=bap, func=AF.Sqrt, scale=1.0 / D, bias=cf[:, C_EPS:C_EPS + 1]))
            P.op("dve", [], [Rr], lambda h: h.reciprocal(out=rstd[:, sl], in_=rstd[:, sl]))
            for k in range(KC):
                dr, dap = dst_fn(k)
                P.op("dve", [XS[k], Rr, R_const], [dr],
                     lambda h: h.scalar_tensor_tensor(out=dap, in0=xs[:, k, sl], scalar=cf[:, gcol + k:gcol + k + 1], in1=rstd[:, sl], op0=ALU.mult, op1=ALU.mult))
                if after_k is not None:
                    after_k(k)

        def rmsnorm(cm, gcol, dst_fn, ncols=T):
            for k in range(KC):
                stats_chunk(cm, k, ncols)
            norm_finish(cm, gcol, dst_fn, ncols)

        def ffn(cm, b_base, HID0):
            xs, XS, cp, CP = cm["xs"], cm["XS"], cm["cp"], cm["CP"]
            for half in range(2):
                for j in range(11):
                    jb = half * 11 + j
                    wr, wap = wnext(b_base + half * 19 + j)
                    w3 = wap.rearrange("p (k c) -> p k c", k=KC)
                    for sub in range(2):
                        gr, gap = bank()
                        mm(gr, gap, [(w3[:, k, sub * 128:(sub + 1) * 128], cp[:, k, :]) for k in range(KC)], [wr] + CP[0:KC])
                        ur, uap = bank()
                        mm(ur, uap, [(w3[:, k, 256 + sub * 128:256 + (sub + 1) * 128], cp[:, k, :]) for k in range(KC)], [wr] + CP[0:KC])
                        tr, tap = tmpf(cm)
                        P.op("act", [gr], [tr], lambda h, gap=gap, tap=tap: h.activation(out=tap, in_=gap, func=AF.Silu))
                        hc = HID0 + j * 2 + sub
                        P.op("dve", [tr, ur], [CP[hc]], lambda h, tap=tap, uap=uap, hc=hc: h.tensor_tensor(out=cp[:, hc, :], in0=uap, in1=tap, op=ALU.mult))
                for n2 in range(8):
                    wr, wap = wnext(b_base + half * 19 + 11 + n2)
                    w3 = wap[:, 0:HF * 256].rearrange("p (k c) -> p k c", k=HF)
                    for sub in range(2):
                        n = n2 * 2 + sub
                        yr, yap = bank()
                        mm(yr, yap, [(w3[:, k, sub * 128:(sub + 1) * 128], cp[:, HID0 + k, :]) for k in range(HF)], [wr] + CP[HID0:HID0 + HF])
                        P.op("dve", [yr], [XS[n]], lambda h, yap=yap, n=n: h.scalar_tensor_tensor(out=xs[:, n, :], in0=yap, scalar=0.5, in1=xs[:, n, :], op0=ALU.mult, op1=ALU.add))
                        if half == 1:
                            stats_delayed(cm, n)

        P.dma("sp", [], [R_const], R_const, cf, cfd)
        o_tmpcb = alloc(NCB * 4)
        cbtmp = view(o_tmpcb, NCB * 4, F32)
        R_cbt = P.res("cbtmp")
        P.dma("sp", [], [R_cbt], R_cbt, cbtmp, cbd)
        P.op("dve", [R_cbt], [R_const], lambda h: h.tensor_copy(out=cb, in_=cbtmp))
        P.op("dve", [], [R_const], lambda h: h.memset(ones, 1.0))
        P.op("dve", [], [R_const], lambda h: h.memset(ones_f, 1.0))
        P.op("dve", [], [R_const], lambda h: h.memset(ones_l, 1.0))
        P.op("dve", [], [R_const], lambda h: h.tensor_scalar(out=ones_f[0:64, :], in0=ones_f[0:64, :], scalar1=cf[0:64, C_VPREV:C_VPREV + 1], scalar2=None, op0=ALU.mult))
        P.op("dve", [], [R_const], lambda h: h.tensor_scalar(out=ones_l[64:128, :], in0=ones_l[64:128, :], scalar1=cf[64:128, C_VNEXT:C_VNEXT + 1], scalar2=None, op0=ALU.mult))
        m4 = mask4.rearrange("p (h a q) -> p h a q", h=4, a=2)
        for hh in range(4):
            P.op("dve", [], [R_const], lambda h, hh=hh: h.tensor_copy(out=m4[:, hh, :, :], in_=cb[:, CB_MA:CB_MA + 256].rearrange("p (a q) -> p a q", a=2)))
        P.barrier()
        state["off"] = phase_mark

        setup_wslots()
        cm = setup_common(38)
        xs, XS, cp, CP = cm["xs"], cm["XS"], cm["cp"], cm["CP"]
        HID0 = 16
        o = alloc(T * 4)
        posi = view(o, T * 4, I32)
        R_pos = P.res("pos")
        o = alloc(T * 4)
        cosT = view(o, T * 4, F32)
        o = alloc(T * 4)
        sinT = view(o, T * 4, F32)
        R_cs = P.res("cossin")
        o = alloc(T * 4)
        tint = view(o, T * 4, I32)
        R_ti = P.res("tint")
        stg = []
        Rstg = []
        for i in range(3):
            o = alloc(4 * T * 2)
            stg.append(view(o, 4 * T * 2))
            Rstg.append(P.res("stg%d" % i))
        o = alloc(4 * 1024 * 2)
        ustage = view(o, 4 * 1024 * 2).rearrange("p (a c) -> p a c", a=4)
        R_us = P.res("ustage")
        R_x1st = P.res("x1store")

        R_pk = [P.res("pack0"), P.res("pack1")]

        def pack_pkg(ht):
            P._wait("pool", [(r.dkey, r.dcount) for r in Rstg + [R_us] if r.dcount])
            pkf = pkgs[ht].rearrange("r c -> (r c)")
            for t in range(2):
                for g in range(3):
                    d = GD[g]
                    src = qkvs[1 + t][g].rearrange("p (h r l) -> p h r l", h=4, r=d)
                    l0 = 64 if ht == 0 else LH[g] - 128
                    o0 = (t * 21 + PKG_RR0[g]) * PKG_G
                    dstv = pkf[o0:o0 + d * PKG_G].rearrange("(p h r l) -> p h r l", p=128, h=4, r=d)
                    for hh in range(4):
                        P.dma("pool", [], [], R_pk[ht], dstv[:, hh, :, :], src[:, hh, :, l0:l0 + 64])
            ut0 = 0 if ht == 0 else 15
            P.dma("pool", [], [], R_pk[ht], pkf[PKG_U:PKG_U + 131072].rearrange("(p c) -> p c", p=128), us[ut0 * 128:(ut0 + 1) * 128, :])

        def send_pkg(ht):
            P._wait("pool", [(R_pk[ht].dkey, R_pk[ht].dcount)])
            nc.gpsimd.collective_compute("AllGather", ALU.bypass, replica_groups=[list(range(NCORE))], ins=[pkgs[ht]], outs=[gats[ht]]).then_inc(cc_sem, 1)

        sched1 = []
        for m in range(NT):
            sched1 += [B_FFN1 + i for i in range(38)]
            sched1 += [B_WIN1 + i for i in range(11)]
        wstart(sched1)
        wsl["pace"] = 1
        xT3 = xT.rearrange("(k p) t -> p k t", p=128)
        x1s3 = x1s.rearrange("(k p) t -> p k t", p=128)
        outT3 = outT.rearrange("(k p) t -> p k t", p=128)
        stgi = 0
        for m in range(NT if STOP_AFTER >= 1 else 0):
            tsl = slice(m * T, (m + 1) * T)
            if m == 3:
                send_pkg(0)
            for k in range(KC):
                P.dma("sp", [], [XS[k]], XS[k], xs[:, k, :], xT3[:, k, tsl])
            rmsnorm(cm, C_GAIN + 0, lambda k: (CP[k], cp[:, k, :]))
            ffn(cm, B_FFN1, HID0)
            P.dma("pool", XS, [], R_x1st, x1s3[:, :, tsl], xs)
            norm_finish(cm, C_GAIN + 16, lambda k: (CP[k], cp[:, k, :]))
            t0r, t0 = tmpf(cm)
            t1r, t1 = tmpf(cm)
            P.dma("sp", [], [R_pos], R_pos, posi, posd[:, tsl])
            P.op("dve", [R_pos], [t0r], lambda h: h.tensor_copy(out=t0, in_=posi))
            P.op("dve", [R_const], [t0r], lambda h: h.tensor_scalar(out=t0, in0=t0, scalar1=cf[:, C_INVF:C_INVF + 1], scalar2=None, op0=ALU.mult))
            P.op("dve", [t0r], [R_ti], lambda h: h.tensor_copy(out=tint, in_=t0))
            P.op("dve", [R_ti], [t1r], lambda h: h.tensor_copy(out=t1, in_=tint))
            P.op("dve", [t1r], [t0r], lambda h: h.tensor_tensor(out=t0, in0=t0, in1=t1, op=ALU.subtract))

            def wrap(buf, bufr, scr, scrr):
                P.op("dve", [bufr], [scrr], lambda h: h.tensor_scalar(out=scr, in0=buf, scalar1=0.5, scalar2=None, op0=ALU.is_gt))
                P.op("dve", [scrr], [bufr], lambda h: h.tensor_tensor(out=buf, in0=buf, in1=scr, op=ALU.subtract))
                P.op("dve", [bufr], [scrr], lambda h: h.tensor_scalar(out=scr, in0=buf, scalar1=-0.5, scalar2=None, op0=ALU.is_lt))
                P.op("dve", [scrr], [bufr], lambda h: h.tensor_tensor(out=buf, in0=buf, in1=scr, op=ALU.add))
            wrap(t0, t0r, t1, t1r)
            P.op("act", [t0r], [R_cs], lambda h: h.activation(out=sinT, in_=t0, func=AF.Sin, scale=6.2831845))
            P.op("dve", [], [t0r], lambda h: h.tensor_scalar(out=t0, in0=t0, scalar1=0.25, scalar2=None, op0=ALU.add))
            wrap(t0, t0r, t1, t1r)
            P.op("act", [t0r], [R_cs], lambda h: h.activation(out=cosT, in_=t0, func=AF.Sin, scale=6.2831845))
            P.op("dve", [R_const], [R_cs], lambda h: h.tensor_scalar(out=sinT, in0=sinT, scalar1=cf[:, C_SSIGN:C_SSIGN + 1], scalar2=None, op0=ALU.mult))
            for bq in range(2):
                wr, wap = wnext(B_WIN1 + bq)
                w3 = wap.rearrange("p (k c) -> p k c", k=KC)
                for tt in range(4):
                    br, bap = bank()
                    mm(br, bap, [(cp[:, k, tt * 128:(tt + 1) * 128], w3[:, k, :]) for k in range(KC)], [wr] + CP[0:KC])
                    P.op("act", [br], [R_us], lambda h, bap=bap, tt=tt, bq=bq: h.activation(out=ustage[:, tt, bq * 512:(bq + 1) * 512], in_=bap, func=AF.Copy))
            P.dma("pool", [R_us], [], R_us, us[m * 512:(m + 1) * 512, :].rearrange("(a p) c -> p a c", p=128), ustage)
            for g in range(3):
                d = GD[g]
                L = T // d
                for t in range(3):
                    wr, wap = wnext(B_WIN1 + 2 + g * 3 + t)
                    w3 = wap.rearrange("p (k c) -> p k c", k=KC)
                    sr, sap = Rstg[stgi % 3], stg[stgi % 3]
                    stgi += 1
                    s4 = sap.rearrange("p (h r l) -> p h r l", h=4, r=d)
                    for hh in range(4):
                        br, bap = bank()
                        mm(br, bap, [(w3[:, k, hh * 128:(hh + 1) * 128], cp[:, k, :]) for k in range(KC)], [wr] + CP[0:KC])
                        bperm = bap.rearrange("p (l r) -> p r l", r=d)
                        if t == 2:
                            P.op("act", [br], [sr], lambda h, bperm=bperm, hh=hh, s4=s4: h.activation(out=s4[:, hh, :, :], in_=bperm, func=AF.Copy))
                        else:
                            ar, aap = tmpf(cm)
                            b2r, b2ap = tmpf(cm)
                            P.op("dve", [br, R_cs], [ar], lambda h, bap=bap, aap=aap: h.tensor_tensor(out=aap, in0=bap, in1=cosT, op=ALU.mult))
                            P.op("dve", [br, R_cs], [b2r], lambda h, bap=bap, b2ap=b2ap: h.tensor_tensor(out=b2ap[0:64, :], in0=bap[64:128, :], in1=sinT[0:64, :], op=ALU.mult))
                            P.op("dve", [br, R_cs], [b2r], lambda h, bap=bap, b2ap=b2ap: h.tensor_tensor(out=b2ap[64:128, :], in0=bap[0:64, :], in1=sinT[64:128, :], op=ALU.mult))
                            P.op("pool", [ar, b2r], [sr], lambda h, aap=aap, b2ap=b2ap, hh=hh, s4=s4, d=d: h.tensor_tensor(
                                out=s4[:, hh, :, :], in0=aap.rearrange("p (l r) -> p r l", r=d), in1=b2ap.rearrange("p (l r) -> p r l", r=d), op=ALU.add))
                    dst = qkvs[t][g].rearrange("p (h r l) -> p h r l", h=4, r=d)
                    off = (0 if t == 0 else 64) + m * L
                    if d == 16:
                        for hh in range(4):
                            P.dma("pool", [sr], [], sr, dst[:, hh, :, off:off + L], s4[:, hh, :, :])
                    else:
                        P.dma("pool", [sr], [], sr, dst[:, :, :, off:off + L], s4)
            if m == 1:
                pack_pkg(0)
        wsl["pace"] = 0
        conv_upto(115)
        P.barrier()

        if STOP_AFTER >= 2:
            pack_pkg(1)
            send_pkg(1)

        if STOP_AFTER >= 3:
            state["off"] = phase_mark
            o_q = alloc(4 * 2048 * 2)
            o_k = alloc(4 * 4096 * 2)
            o_v = alloc(4 * 4096 * 2)
            o_num = alloc(4 * TPC * 4)
            o_den = alloc(4 * TPC * 4)
            numacc = view(o_num, 4 * TPC * 4, F32)
            denacc = view(o_den, 4 * TPC * 4, F32)
            R_acc = P.res("acc")
            R_q, R_k, R_v = P.res("Qt"), P.res("Kt"), P.res("Vt")
            cand = []
            Rcand = []
            for i in range(2):
                o = alloc(4 * 16 * 64 * 2)
                cand.append(view(o, 4 * 16 * 64 * 2))
                Rcand.append(P.res("cand%d" % i))
            Pt = []
            RPt = []
            for i in range(2):
                o = alloc(1024 * 2)
                Pt.append(view(o, 1024 * 2))
                RPt.append(P.res("Pt%d" % i))
            Vtok = []
            RVt = []
            for i in range(4):
                o = alloc(512 * 2)
                Vtok.append(view(o, 512 * 2).rearrange("p (h c) -> p h c", h=4))
                RVt.append(P.res("Vtok%d" % i))
            o = alloc(16 * 128 * 2)
            dg = view(o, 16 * 128 * 2).rearrange("p (a c) -> p a c", a=16)
            R_dg = P.res("diag")
            for side in range(2):
                for s in range(NCORE):
                    fcol = (C_FPREV if side == 0 else C_FNEXT) + s
                    P.op("dve", [R_const], [R_dg], lambda h, side=side, s=s, fcol=fcol: h.tensor_scalar(out=dg[:, side * 8 + s, :], in0=ident, scalar1=cf[:, fcol:fcol + 1], scalar2=None, op0=ALU.mult))
            gatfs = [gt_.rearrange("r c -> (r c)") for gt_ in gats]
            cstate = {"ci": 0}

            def select(side, ncols, src_fn, dst_fn, dst_res):
                nch = (ncols + 511) // 512
                if not cstate.get("ccw"):
                    P.eng["sp"]["h"].wait_ge(cc_sem, 2)
                    cstate["ccw"] = True
                for s in range(NCORE):
                    cr, cap = Rcand[cstate["ci"] % 2], cand[cstate["ci"] % 2]
                    cstate["ci"] += 1
                    P.dma("sp", [], [cr], cr, cap[:, 0:ncols], src_fn(s))
                    for c in range(nch):
                        n = min(512, ncols - c * 512)
                        P.op("pe", [cr, R_dg], [PB[c]], lambda h, c=c, n=n, s=s, cap=cap: h.matmul(
                            ps[:, c * 512:c * 512 + n], lhsT=dg[:, side * 8 + s, :], rhs=cap[:, c * 512:c * 512 + n], start=(s == 0), stop=(s == NCORE - 1)))
                for c in range(nch):
                    n = min(512, ncols - c * 512)
                    if c % 2 == 0:
                        P.op("act", [PB[c]], [dst_res], lambda h, c=c, n=n: h.activation(out=dst_fn(c * 512, n), in_=dst_shape(ps[:, c * 512:c * 512 + n]), func=AF.Copy))
                    else:
                        P.op("dve", [PB[c]], [dst_res], lambda h, c=c, n=n: h.tensor_copy(out=dst_fn(c * 512, n), in_=dst_shape(ps[:, c * 512:c * 512 + n])))
                pbi["i"] = 0

            def u_halo():
              for side in range(2):
                select(side, 1024,
                       lambda s, side=side: gatfs[1 - side][s * PKG_N + PKG_U: s * PKG_N + PKG_U + 131072].rearrange("(p c) -> p c", p=128),
                       lambda c0, n, side=side: uhalo[:, side, c0:c0 + n], R_uh)
            pti = 0
            unit_ctr = {"n": 0}
            vctr = {"n": 0}
            for g in range(3):
                d = GD[g]
                Lh = LH[g]
                Lq = TPC // d
                nq = Lq // 128
                Qt = view(o_q, 4 * d * Lq * 2).rearrange("p (h r l) -> p h r l", h=4, r=d)
                Kt = view(o_k, 4 * d * Lh * 2).rearrange("p (h r l) -> p h r l", h=4, r=d)
                Vt = view(o_v, 4 * d * Lh * 2).rearrange("p (h r l) -> p h r l", h=4, r=d)
                P.dma("sp", [], [R_q], R_q, view(o_q, 4 * d * Lq * 2), qkvs[0][g])
                P.dma("sp", [], [R_k], R_k, view(o_k, 4 * d * Lh * 2), qkvs[1][g])
                P.dma("sp", [], [R_v], R_v, view(o_v, 4 * d * Lh * 2), qkvs[2][g])
                if g == 0:
                    dst_shape = lambda a: a
                    u_halo()
                dst_shape = lambda a: a.rearrange("p (a l) -> p a l", l=64)
                for side in range(2):
                    for t in range(2):
                        Xt, Rx = (Kt, R_k) if t == 0 else (Vt, R_v)
                        l0 = 0 if side == 0 else Lh - 64
                        dsth = Xt[:, :, :, l0:l0 + 64].rearrange("p h r l -> p (h r) l")
                        ht = 1 - side
                        o0 = (t * 21 + PKG_RR0[g]) * PKG_G
                        select(side, 4 * d * 64,
                               lambda s, o0=o0, d=d, ht=ht: gatfs[ht][s * PKG_N + o0: s * PKG_N + o0 + d * PKG_G].rearrange("(p c) -> p c", p=128),
                               lambda c0, n, dsth=dsth: dsth[:, c0 // 64:(c0 + n) // 64, :], Rx)
                na4 = numacc.rearrange("p (h l r) -> p h l r", h=4, r=d)
                da4 = denacc.rearrange("p (h l r) -> p h l r", h=4, r=d)

                def stageA(r, jq, vt_of):
                    def get_vtok(kt):
                        if kt in vt_of:
                            return vt_of[kt]
                        i = vctr["n"] % 4
                        vctr["n"] += 1
                        br, bap = bank()
                        b3 = bap.rearrange("p (h c) -> p h c", h=4)

                        def fn(h):
                            ins = None
                            for hh in range(4):
                                ins = h.matmul(b3[:, hh, :], lhsT=Vt[:, hh, r, kt * 128:(kt + 1) * 128], rhs=ident, start=True, stop=True)
                            return ins
                        P.op("pe", [R_v, R_const], [br], fn)
                        P.op("act", [br], [RVt[i]], lambda h: h.activation(out=Vtok[i], in_=b3, func=AF.Copy))
                        vt_of[kt] = i
                        return i
                    unit_ctr["n"] += 1
                    va = get_vtok(jq)
                    vb = get_vtok(jq + 1)
                    brs, bap = bank2()
                    b4 = bap.rearrange("p (h a q) -> p h a q", h=4, a=2)

                    def fnS(h):
                        ins = None
                        for hh in range(4):
                            for ab in range(2):
                                ins = h.matmul(b4[:, hh, ab, :], lhsT=Kt[:, hh, r, (jq + ab) * 128:(jq + ab + 1) * 128], rhs=Qt[:, hh, r, jq * 128:(jq + 1) * 128], start=True, stop=True)
                        return ins
                    P.op("pe", [R_k, R_q], brs, fnS)
                    k_ = unit_ctr["n"] % 2
                    pr, pap = RPt[k_], Pt[k_]
                    P.op("act", brs, [pr], lambda h: h.activation(out=pap, in_=bap, func=AF.Exp, scale=float(128 ** -0.5)))
                    P.op("pool", [R_const], [pr], lambda h: h.tensor_tensor(out=pap, in0=pap, in1=mask4, op=ALU.mult))
                    return dict(r=r, jq=jq, va=va, vb=vb, pr=pr, pap=pap)

                def stageB(cx):
                    r, jq, va, vb, pr, pap = cx["r"], cx["jq"], cx["va"], cx["vb"], cx["pr"], cx["pap"]
                    p4 = pap.rearrange("p (h a q) -> p h a q", h=4, a=2)
                    nr, nap = bank()
                    n3 = nap.rearrange("p (h q) -> p h q", h=4)

                    def fnN(h):
                        ins = None
                        for hh in range(4):
                            ins = h.matmul(n3[:, hh, :], lhsT=Vtok[va][:, hh, :], rhs=p4[:, hh, 0, :], start=True, stop=False)
                            ins = h.matmul(n3[:, hh, :], lhsT=Vtok[vb][:, hh, :], rhs=p4[:, hh, 1, :], start=False, stop=True)
                        return ins
                    P.op("pe", [pr, RVt[va], RVt[vb]], [nr], fnN)
                    dr_, dap = bank()
                    d3 = dap.rearrange("p (h q) -> p h q", h=4)
                    oa = ones_f if jq == 0 else ones
                    ob = ones_l if jq + 1 == nq else ones

                    def fnD(h):
                        h.matmul(d3, lhsT=oa, rhs=p4[:, :, 0, :], start=True, stop=False)
                        return h.matmul(d3, lhsT=ob, rhs=p4[:, :, 1, :], start=False, stop=True)
                    P.op("pe", [pr, R_const], [dr_], fnD)
                    nsl = na4[:, :, jq * 128:(jq + 1) * 128, r]
                    dsl = da4[:, :, jq * 128:(jq + 1) * 128, r]
                    if g == 0:
                        P.op("dve", [nr, R_acc], [], lambda h: h.tensor_copy(out=nsl, in_=n3))
                        P.op("act", [dr_, R_acc], [], lambda h: h.activation(out=dsl, in_=d3, func=AF.Copy))
                    else:
                        P.op("dve", [nr, R_acc], [], lambda h: h.tensor_tensor(out=nsl, in0=n3, in1=nsl, op=ALU.add))
                        P.op("dve", [dr_, R_acc], [], lambda h: h.tensor_tensor(out=dsl, in0=d3, in1=dsl, op=ALU.add))

                pend = None
                for r in range(d):
                    vt_of = {}
                    for jq in range(nq):
                        cx = stageA(r, jq, vt_of)
                        if pend is not None:
                            stageB(pend)
                        pend = cx
                stageB(pend)
                P.op("dve", [R_q, R_k, R_v], [R_acc], lambda h: h.memset(cand[0][:, 0:2], 0.0))
                P.op("dve", [], [R_q, R_k, R_v, Rcand[0]], lambda h: h.memset(cand[0][:, 0:2], 0.0))
            aost = view(o_k, 4 * TPC * 2)
            P.op("dve", [], [R_acc], lambda h: h.reciprocal(out=denacc, in_=denacc))
            P.op("dve", [], [R_acc, R_k], lambda h: h.tensor_tensor(out=aost, in0=numacc, in1=denacc, op=ALU.mult))
            P.dma("sp", [R_k], [], R_k, aos, aost)
            P.barrier()

        if STOP_AFTER >= 4:
            state["off"] = phase_mark
            setup_wslots()
            cm = setup_common()
            xs, XS, cp, CP = cm["xs"], cm["XS"], cm["cp"], cm["CP"]
            HID0 = 16
            QM0, PL0, MX0, MG0, OM0 = 16, 24, 32, 40, 56
            UT0 = 40
            gt = []
            Rgt = []
            for i in range(3):
                o = alloc(T * 4)
                gt.append(view(o, T * 4, F32))
                Rgt.append(P.res("gt%d" % i))
            o = alloc(4 * T * 2)
            aoT = view(o, 4 * T * 2).rearrange("p (h t) -> p h t", h=4)
            R_ao = P.res("aoT")
            o = alloc(2 * T * 2)
            Pm = view(o, 2 * T * 2).rearrange("p (a t) -> p a t", a=2)
            R_pm = P.res("Pm")
            R_out = P.res("outst")

            sched3 = [B_MEMKV + i for i in range(4)]
            for m in range(NT):
                sched3 += [B_QMEM, B_QMEM + 1, B_POOLW]
                sched3 += [B_GU3 + i for i in range(32)]
                sched3 += [B_WOUT + i for i in range(4)]
                sched3 += [B_FFN2 + i for i in range(38)]
            conv_upto(B_FFN2 + 1)
            wstart(sched3)
            wsl["pace"] = 3
            memT3 = memTd.rearrange("(k p) t -> p k t", p=128)
            P.dma("sp", [], XS, XS[0], xs[:, :, 0:256], memT3)
            rmsnorm(cm, C_GAIN + 64, lambda k: (CP[k], cp[:, k, 0:256]), ncols=256)
            for j in range(2):
                wr, wap = wnext(B_MEMKV + j)
                w3 = wap.rearrange("p (k c) -> p k c", k=KC)
                for q4 in range(4):
                    br, bap = bank()
                    mm(br, bap[:, 0:256], [(w3[:, k, q4 * 128:(q4 + 1) * 128], cp[:, k, 0:256]) for k in range(KC)], [wr] + CP[0:KC])
                    P.op("act", [br], [R_km], lambda h, bap=bap, j=j, q4=q4: h.activation(out=kmT[:, j * 4 + q4, :], in_=bap[:, 0:256], func=AF.Copy))
            for j in range(2):
                wr, wap = wnext(B_MEMKV + 2 + j)
                w3 = wap.rearrange("p (k c) -> p k c", k=KC)
                for mt in range(2):
                    br, bap = bank()
                    mm(br, bap, [(cp[:, k, mt * 128:(mt + 1) * 128], w3[:, k, :]) for k in range(KC)], [wr] + CP[0:KC])
                    P.op("act", [br], [R_km], lambda h, bap=bap, j=j, mt=mt: h.activation(out=vm[:, mt, j * 512:(j + 1) * 512], in_=bap, func=AF.Copy))
            bands = cb[:, CB_BAND:CB_BAND + 4608].rearrange("p (w v n t) -> p w v n t", w=4, v=3, n=3)
            icnt = cf[:, C_ICNT:C_ICNT + 1536].rearrange("p (w v t) -> p w v t", w=4, v=3)
            for m in range(NT):
                tsl = slice(m * T, (m + 1) * T)
                for k in range(KC):
                    P.dma("sp", [], [XS[k]], XS[k], xs[:, k, :], x1s3[:, k, tsl])
                P.dma("sp", [], [R_ao], R_ao, aoT, aos.rearrange("p (h t) -> p h t", h=4)[:, :, tsl])
                lo = max(0, 4 * m - 1)
                hi = min(15, 4 * m + 4)
                s0 = lo - (4 * m - 1)
                nt_ = hi - lo + 1
                ut = cp[:, UT0:UT0 + 12, :].rearrange("p (a b) t -> p a (b t)", a=6)
                P.dma("sp", [], CP[UT0 + 2 * s0:UT0 + 2 * (s0 + nt_)], CP[UT0], ut[:, s0:s0 + nt_, :], us[lo * 128:(hi + 1) * 128, :].rearrange("(a p) c -> p a c", p=128))
                rmsnorm(cm, C_GAIN + 16, lambda k: (CP[k], cp[:, k, :]))
                for j in range(2):
                    wr, wap = wnext(B_QMEM + j)
                    w3 = wap.rearrange("p (k c) -> p k c", k=KC)
                    for q4 in range(4):
                        br, bap = bank()
                        mm(br, bap, [(w3[:, k, q4 * 128:(q4 + 1) * 128], cp[:, k, :]) for k in range(KC)], [wr] + CP[0:KC])
                        c = QM0 + j * 4 + q4
                        P.op("act", [br], [CP[c]], lambda h, bap=bap, c=c: h.activation(out=cp[:, c, :], in_=bap, func=AF.Copy))
                for hm in range(4):
                    brs, bap = bank2()
                    b3 = bap.rearrange("p (a t) -> p a t", a=2)

                    def fnS(h, hm=hm, b3=b3):
                        ins = None
                        for mt in range(2):
                            for c2 in range(2):
                                ins = h.matmul(b3[:, mt, :], lhsT=kmT[:, hm * 2 + c2, mt * 128:(mt + 1) * 128], rhs=cp[:, QM0 + hm * 2 + c2, :], start=(c2 == 0), stop=(c2 == 1))
                        return ins
                    P.op("pe", [R_km, CP[QM0 + hm * 2], CP[QM0 + hm * 2 + 1]], brs, fnS)
                    P.op("act", brs, [R_pm], lambda h, b3=b3: h.activation(out=Pm, in_=b3, func=AF.Exp, scale=float(256 ** -0.5)))
                    dr_, dap = bank()
                    mm(dr_, dap, [(ones, Pm[:, mt, :]) for mt in range(2)], [R_pm, R_const])
                    rr, rap = tmpf(cm)
                    P.op("dve", [dr_], [rr], lambda h, dap=dap, rap=rap: h.reciprocal(out=rap, in_=dap))
                    for c2 in range(2):
                        orr, oap = bank()
                        mm(orr, oap, [(vm[:, mt, hm * 256 + c2 * 128:hm * 256 + (c2 + 1) * 128], Pm[:, mt, :]) for mt in range(2)], [R_pm, R_km])
                        c = OM0 + hm * 2 + c2
                        P.op("dve", [orr, rr], [CP[c]], lambda h, oap=oap, rap=rap, c=c: h.tensor_tensor(out=cp[:, c, :], in0=oap, in1=rap, op=ALU.mult))
                for tt in range(4):
                    i_t = 4 * m + tt
                    var = 0 if i_t == 0 else (2 if i_t == 15 else 1)
                    brs, bap = bank2()
                    b4 = bap.rearrange("p (w c t) -> p w c t", w=4, c=2)

                    def usrc(nb, ch, i_t=i_t, m=m):
                        ti = i_t + nb - 1
                        if ti < 0:
                            return uhalo[:, 0, ch * 128:(ch + 1) * 128], R_uh
                        if ti > 15:
                            return uhalo[:, 1, ch * 128:(ch + 1) * 128], R_uh
                        sl_ = ti - (4 * m - 1)
                        return ut[:, sl_, ch * 128:(ch + 1) * 128], CP[UT0 + 2 * sl_ + (ch // 4)]
                    rd = set()
                    plist = []
                    for wg in range(4):
                        for c2 in range(2):
                            ch = wg * 2 + c2
                            for nb in range(3):
                                ap_, r_ = usrc(nb, ch)
                                rd.add(r_)
                                plist.append((b4[:, wg, c2, :], ap_, bands[:, wg, var, nb, :], nb))

                    def fnP(h, plist=plist):
                        ins = None
                        for (o_, l_, r_, nb) in plist:
                            ins = h.matmul(o_, lhsT=l_, rhs=r_, start=(nb == 0), stop=(nb == 2))
                        return ins
                    P.op("pe", list(rd) + [R_const], brs, fnP)
                    pl = cp[:, PL0:PL0 + 8, tt * 128:(tt + 1) * 128].rearrange("p (w c) t -> p w c t", w=4)
                    for c2 in range(2):
                        P.op("dve", brs + [R_const], CP[PL0:PL0 + 8], lambda h, b4=b4, pl=pl, var=var, c2=c2: h.tensor_tensor(out=pl[:, :, c2, :], in0=b4[:, :, c2, :], in1=icnt[:, :, var, :], op=ALU.mult))
                wr, wap = wnext(B_POOLW)
                pw4 = wap[:, 0:2048].rearrange("p (g k c) -> p g k c", g=4, k=2)
                for wg in range(4):
                    for dc in range(2):
                        br, bap = bank()
                        mm(br, bap, [(pw4[:, wg, k2, dc * 128:(dc + 1) * 128], cp[:, PL0 + wg * 2 + k2, :]) for k2 in range(2)], [wr] + CP[PL0 + wg * 2:PL0 + wg * 2 + 2])
                        c = wg * 2 + dc
                        P.op("act", [br, R_const], [CP[MX0 + c]], lambda h, bap=bap, c=c: h.activation(out=cp[:, MX0 + c, :], in_=bap, func=AF.Identity, scale=cf[:, C_PS + c:C_PS + c + 1]))
                for n2 in range(8):
                    for sub in range(2):
                        c = n2 * 2 + sub
                        wr, wap = wnext(B_GU3 + c * 2)
                        w3 = wap[:, 0:KC * 384].rearrange("p (k c) -> p k c", k=KC)
                        for i in range(3):
                            br, bap = bank()
                            mm(br, bap, [(w3[:, k, i * 128:(i + 1) * 128], cp[:, k, :]) for k in range(KC)], [wr] + CP[0:KC])
                            P.op("act", [br, R_const], [Rgt[i]], lambda h, bap=bap, i=i, c=c: h.activation(out=gt[i], in_=bap, func=AF.Sigmoid, bias=cf[:, C_BG + i * 16 + c:C_BG + i * 16 + c + 1]))
                        ur_, uap_ = wnext(B_GU3 + c * 2 + 1)
                        u3 = uap_[:, 0:20 * 128].rearrange("p (k c) -> p k c", k=20)
                        ucol = slice(0, 128)
                        y0r, y0 = bank()
                        mm(y0r, y0, [(u3[:, k, ucol], cp[:, MX0 + k, :]) for k in range(8)], [ur_] + CP[MX0:MX0 + 8])
                        t0r, t0 = tmpf(cm)
                        P.op("dve", [y0r, Rgt[0]], [t0r], lambda h, y0=y0, t0=t0: h.tensor_tensor(out=t0, in0=y0, in1=gt[0], op=ALU.mult))
                        y1r, y1 = bank()
                        mm(y1r, y1, [(u3[:, 8 + k, ucol], aoT[:, k, :]) for k in range(4)], [ur_, R_ao])
                        t1r, t1 = tmpf(cm)
                        P.op("dve", [y1r, Rgt[1]], [t1r], lambda h, y1=y1, t1=t1: h.tensor_tensor(out=t1, in0=y1, in1=gt[1], op=ALU.mult))
                        P.op("pool", [t1r], [t0r], lambda h, t0=t0, t1=t1: h.tensor_tensor(out=t0, in0=t0, in1=t1, op=ALU.add))
                        y2r, y2 = bank()
                        mm(y2r, y2, [(u3[:, 12 + k, ucol], cp[:, OM0 + k, :]) for k in range(8)], [ur_] + CP[OM0:OM0 + 8])
                        t2r, t2 = tmpf(cm)
                        P.op("dve", [y2r, Rgt[2]], [t2r], lambda h, y2=y2, t2=t2: h.tensor_tensor(out=t2, in0=y2, in1=gt[2], op=ALU.mult))
                        P.op("pool", [t0r, t2r], [CP[MG0 + c]], lambda h, t0=t0, t2=t2, c=c: h.tensor_tensor(out=cp[:, MG0 + c, :], in0=t0, in1=t2, op=ALU.add))
                for j in range(4):
                    wr, wap = wnext(B_WOUT + j)
                    w3 = wap.rearrange("p (k c) -> p k c", k=KC)
                    for q4 in range(4):
                        n = j * 4 + q4
                        br, bap = bank()
                        mm(br, bap, [(w3[:, k, q4 * 128:(q4 + 1) * 128], cp[:, MG0 + k, :]) for k in range(KC)], [wr] + CP[MG0:MG0 + KC])
                        P.op("dve", [br], [XS[n]], lambda h, bap=bap, n=n: h.tensor_tensor(out=xs[:, n, :], in0=bap, in1=xs[:, n, :], op=ALU.add))
                        stats_delayed(cm, n)
                norm_finish(cm, C_GAIN + 32, lambda k: (CP[k], cp[:, k, :]))
                ffn(cm, B_FFN2, HID0)
                norm_finish(cm, C_GAIN + 48, lambda k: (XS[k], xs[:, k, :]),
                            after_k=lambda k: P.dma("pool", [XS[k]], [], R_out, outT3[:, k, tsl], xs[:, k, :]))
        P.barrier()
    return nc


_CACHE = {}


def kernel(**inp):
    inp = {k: np.asarray(v) for k, v in inp.items()}
    wallh = build_wall(inp)
    x = inp["x"][0]
    memT = np.ascontiguousarray(inp["mem"][0].T.astype(np.float32))
    pos = inp["positions"][0].astype(np.int32)
    in_maps = []
    for c in range(NCORE):
        cfh, cbh = build_consts(inp, c)
        xTc = np.ascontiguousarray(x[c * TPC:(c + 1) * TPC, :].T)
        posc = np.ascontiguousarray(np.broadcast_to(pos[c * TPC:(c + 1) * TPC][None, :], (128, TPC)))
        in_maps.append({"xT": xTc, "wall": wallh, "cf": cfh, "cb": cbh, "pos": posc, "memT": memT})
    if "nc" not in _CACHE:
        _CACHE["nc"] = build_program()
    res = run_bass_kernel_spmd(_CACHE["nc"], in_maps, core_ids=list(range(NCORE)))
    _CACHE["res"] = res
    out = np.empty((1, S, D), np.float32)
    for c in range(NCORE):
        out[0, c * TPC:(c + 1) * TPC, :] = res.results[c]["outT"].T
    return out
```

```python
import os
from contextlib import ExitStack
import numpy as np
import concourse.bass as bass
import concourse.mybir as mybir
from concourse.bass_utils import run_bass_kernel_spmd

F32 = mybir.dt.float32
BF16 = mybir.dt.bfloat16
I32 = mybir.dt.int32
ALU = mybir.AluOpType
AF = mybir.ActivationFunctionType

NCORE = 8
S = 16384
D = 2048
TPC = S // NCORE
T = 512
NT = TPC // T
KC = D // 128
DFF = 5632
FC = DFF // 128
HF = FC // 2
BLK = 8192
GD = (1, 4, 16)
EPS = 1e-6
DEBUG = bool(int(os.environ.get("KDEBUG", "0")))
STOP_AFTER = int(os.environ.get("KSTOP", "9"))

B_FFN1 = 0
B_WIN1 = 38
B_MEMKV = 49
B_QMEM = 53
B_POOLW = 55
B_GU3 = 56
B_WOUT = 88
B_FFN2 = 92
NBLK = 130


def blk_used(b):
    if B_FFN1 <= b < B_FFN1 + 38 or B_FFN2 <= b < B_FFN2 + 38:
        r = (b - (B_FFN1 if b < 38 else B_FFN2)) % 19
        return 8192 if r < 11 else 22 * 256
    if b == B_POOLW:
        return 2048
    if B_GU3 <= b < B_GU3 + 32:
        return 16 * 384 if (b - B_GU3) % 2 == 0 else 20 * 128
    return 8192

C_GAIN = 0
C_BG = 80
C_PS = 128
C_INVF = 136
C_SSIGN = 137
C_FPREV = 138
C_FNEXT = 146
C_VPREV = 154
C_VNEXT = 155
C_EPS = 156
C_ICNT = 160
NCF = 160 + 1536
CB_ID = 0
CB_MA = 128
CB_MB = 256
CB_BAND = 384
NCB = 384 + 4 * 3 * 3 * 128

PKG_G = 128 * 4 * 64
PKG_RR0 = (0, 1, 5)
PKG_U = 2 * 21 * PKG_G
PKG_N = PKG_U + 128 * 1024
PKG_ROWS = PKG_N // 256


def _ffn_blocks(wg, wu, wd):
    out = []
    for j in range(22):
        b = np.concatenate([wg[:, 256 * j:256 * j + 256], wu[:, 256 * j:256 * j + 256]], axis=1)
        out.append(b.reshape(16, 128, 512).transpose(1, 0, 2).reshape(128, 8192))
    for half in range(2):
        for n2 in range(8):
            b = wd[half * 2816:(half + 1) * 2816, n2 * 256:(n2 + 1) * 256]
            bb = np.zeros((128, BLK), np.float32)
            bb[:, :22 * 256] = b.reshape(22, 128, 256).transpose(1, 0, 2).reshape(128, 22 * 256)
            out.append(bb)
    res = []
    for half in range(2):
        res += out[half * 11:(half + 1) * 11] + out[22 + half * 8:22 + (half + 1) * 8]
    return res


def _kblock(w):
    K, C = w.shape
    kc = K // 128
    bb = np.zeros((128, BLK), np.float32)
    bb[:, :kc * C] = w.reshape(kc, 128, C).transpose(1, 0, 2).reshape(128, kc * C)
    return bb


def build_wall(inp):
    f1 = _ffn_blocks(inp["ffn1_w_gate"][0], inp["ffn1_w_up"][0], inp["ffn1_w_down"][0])
    f2 = _ffn_blocks(inp["ffn2_w_gate"][0], inp["ffn2_w_up"][0], inp["ffn2_w_down"][0])
    win = inp["w_in"][0]
    blocks = []
    blocks += f1
    blocks.append(_kblock(win[:, 0:512]))
    blocks.append(_kblock(win[:, 512:1024]))
    for g in range(3):
        for t in range(3):
            c0 = 1024 + t * 1536 + g * 512
            blocks.append(_kblock(win[:, c0:c0 + 512]))
    wkv = inp["w_mem_kv"][0]
    for j in range(4):
        blocks.append(_kblock(wkv[:, j * 512:(j + 1) * 512]))
    blocks.append(_kblock(win[:, 5632:6144]))
    blocks.append(_kblock(win[:, 6144:6656]))
    pw = inp["pool_w"][0]
    bb = np.zeros((128, BLK), np.float32)
    bb[:, :8 * 256] = pw.reshape(4, 2, 128, 256).transpose(2, 0, 1, 3).reshape(128, 8 * 256)
    blocks.append(bb)
    wup = np.concatenate([inp["w_up_pool"][0], inp["w_up_attn"][0], inp["w_up_mem"][0]], axis=0)
    for c in range(16):
        cols = np.concatenate([win[:, 6656 + i * 2048 + c * 128: 6656 + i * 2048 + (c + 1) * 128] for i in range(3)], axis=1)
        blocks.append(_kblock(cols))
        blocks.append(_kblock(wup[:, c * 128:(c + 1) * 128]))
    wo = inp["w_out"][0]
    for j in range(4):
        blocks.append(_kblock(wo[:, j * 512:(j + 1) * 512]))
    blocks += f2
    assert len(blocks) == NBLK
    return np.ascontiguousarray(np.stack(blocks, 0).reshape(NBLK * 128, BLK))


def build_consts(inp, c):
    cf = np.zeros((128, NCF), np.float32)
    for i, nm in enumerate(["ffn1_norm", "mix_norm", "ffn2_norm", "final_norm", "mem_norm"]):
        v = np.asarray(inp[nm], np.float32).reshape(-1)
        cf[:, C_GAIN + 16 * i:C_GAIN + 16 * (i + 1)] = v.reshape(16, 128).T
    cf[:, C_BG:C_BG + 48] = np.asarray(inp["b_gate"], np.float32).reshape(48, 128).T
    cf[:, C_PS:C_PS + 8] = np.asarray(inp["pool_scale"], np.float32).reshape(8, 128).T
    half = 64
    invf = (10000.0 ** (-np.arange(half, dtype=np.float32) / half)).astype(np.float32)
    cf[:, C_INVF] = np.concatenate([invf, invf]) / np.float32(2 * np.pi)
    cf[:, C_EPS] = EPS
    cf[:64, C_SSIGN] = -1.0
    cf[64:, C_SSIGN] = 1.0
    if c > 0:
        cf[:, C_FPREV + c - 1] = 1.0
        cf[:, C_VPREV] = 1.0
    if c < NCORE - 1:
        cf[:, C_FNEXT + c + 1] = 1.0
        cf[:, C_VNEXT] = 1.0
    cb = np.zeros((128, NCB), np.float32)
    cb[:, CB_ID:CB_ID + 128] = np.eye(128, dtype=np.float32)
    i = np.arange(128)[:, None]
    j = np.arange(128)[None, :]
    cb[:, CB_MA:CB_MA + 128] = (i >= j)
    cb[:, CB_MB:CB_MB + 128] = (i <= j)
    for wg, w in enumerate((2, 4, 8, 16)):
        hw = w // 2
        for var, ti in enumerate((0, 1, 15)):
            g0 = c * TPC + ti * 128
            P = g0 + np.arange(128)
            lo = np.clip(P - hw, 0, S)
            hi = np.clip(P + hw + 1, 0, S)
            cnt = (hi - lo).astype(np.float32)
            cf[:, C_ICNT + (wg * 3 + var) * 128:C_ICNT + (wg * 3 + var + 1) * 128] = (1.0 / cnt)[None, :]
            for nb in range(3):
                Pp = g0 + (nb - 1) * 128 + np.arange(128)
                B = ((Pp[:, None] >= lo[None, :]) & (Pp[:, None] < hi[None, :])).astype(np.float32)
                B -= (Pp[:, None] == P[None, :]) * cnt[None, :]
                o = CB_BAND + ((wg * 3 + var) * 3 + nb) * 128
                cb[:, o:o + 128] = B
    return cf, cb


class Res:
    __slots__ = ("name", "w", "r", "dsem", "dkey", "dcount")

    def __init__(self, name):
        self.name = name
        self.w = []
        self.r = {}
        self.dsem = None
        self.dkey = None
        self.dcount = 0


class Prog:
    def __init__(self, nc, stack):
        self.nc = nc
        self.stack = stack
        self.sems = {}
        self.eng = {}
        self.allres = []
        for key, h in (("pe", nc.tensor), ("dve", nc.vector), ("act", nc.scalar), ("pool", nc.gpsimd), ("sp", nc.sync)):
            sem = stack.enter_context(nc.semaphore("e_" + key))
            self.sems[key] = sem
            self.eng[key] = {"h": h, "count": 0, "seen": {}}
        self.nsem = 0

    def res(self, name):
        r = Res(name)
        self.allres.append(r)
        return r

    def _dsem(self, r):
        if r.dsem is None:
            self.nsem += 1
            r.dkey = "d%d_%s" % (self.nsem, r.name)
            r.dsem = self.stack.enter_context(self.nc.semaphore(r.dkey))
            self.sems[r.dkey] = r.dsem
        return r.dsem

    def _wait(self, ek, toks):
        e = self.eng[ek]
        need = {}
        for (k, v) in toks:
            if k == ek and ek in ("pe",):
                continue
            if e["seen"].get(k, 0) < v:
                need[k] = max(need.get(k, 0), v)
        for k, v in need.items():
            e["h"].wait_ge(self.sems[k], v)
            e["seen"][k] = v

    @staticmethod
    def _deps(reads, writes):
        toks = []
        for r in reads:
            toks += r.w
        for w in writes:
            toks += w.w
            toks += list(w.r.items())
        return toks

    @staticmethod
    def _commit(reads, writes, tok):
        for r in reads:
            if r.r.get(tok[0], 0) < tok[1]:
                r.r[tok[0]] = tok[1]
        for w in writes:
            w.w = [tok]
            w.r = {}

    def op(self, ek, reads, writes, fn):
        self._wait(ek, self._deps(reads, writes))
        e = self.eng[ek]
        ins = fn(e["h"])
        e["count"] += 1
        ins.then_inc(self.sems[ek], 1)
        self._commit(reads, writes, (ek, e["count"]))

    def dma(self, qk, reads, writes, semres, out_ap, in_ap):
        self._wait(qk, self._deps(reads, writes))
        sem = self._dsem(semres)
        ins = self.eng[qk]["h"].dma_start(out=out_ap, in_=in_ap)
        semres.dcount += 16
        ins.then_inc(sem, 16)
        self._commit(reads, writes, (semres.dkey, semres.dcount))

    def barrier(self):
        toks = [(k, e["count"]) for k, e in self.eng.items() if e["count"] > 0 and k != "sp"]
        for r in self.allres:
            if r.dsem is not None and r.dcount > 0:
                toks.append((r.dkey, r.dcount))
        for ek in self.eng:
            e = self.eng[ek]
            for (k, v) in toks:
                if k == ek:
                    continue
                if e["seen"].get(k, 0) < v:
                    e["h"].wait_ge(self.sems[k], v)
                    e["seen"][k] = v
        for r in self.allres:
            r.w = [t for t in r.w if t[0] in ("dve", "act", "pool")]
            r.r = {k: v for k, v in r.r.items() if k in ("dve", "act", "pool")}


def build_program():
    nc = bass.Bass("TRN2", target_bir_lowering=False)
    dk = "ExternalOutput" if DEBUG else "Internal"
    xT = nc.dram_tensor("xT", [D, TPC], F32, kind="ExternalInput").ap()
    wall = nc.dram_tensor("wall", [NBLK * 128, BLK], F32, kind="ExternalInput").ap()
    cfd = nc.dram_tensor("cf", [128, NCF], F32, kind="ExternalInput").ap()
    cbd = nc.dram_tensor("cb", [128, NCB], F32, kind="ExternalInput").ap()
    posd = nc.dram_tensor("pos", [128, TPC], I32, kind="ExternalInput").ap()
    memTd = nc.dram_tensor("memT", [D, 256], F32, kind="ExternalInput").ap()
    outT = nc.dram_tensor("outT", [D, TPC], F32, kind="ExternalOutput").ap()
    WSPLIT = 64
    wbf_a = nc.dram_tensor("wbf_a", [WSPLIT * 128, BLK], BF16, kind="Internal").ap()
    wbf_b = nc.dram_tensor("wbf_b", [(NBLK - WSPLIT) * 128, BLK], BF16, kind="Internal").ap()

    def wbf_rows(b0, b1):
        if b1 <= WSPLIT:
            return wbf_a[b0 * 128:b1 * 128, :]
        assert b0 >= WSPLIT
        return wbf_b[(b0 - WSPLIT) * 128:(b1 - WSPLIT) * 128, :]
    x1s = nc.dram_tensor("x1s", [D, TPC], F32, kind=dk).ap()
    us = nc.dram_tensor("us", [16 * 128, 1024], BF16, kind=dk).ap()
    aos = nc.dram_tensor("aos", [128, 4 * TPC], BF16, kind=dk).ap()
    LH = [TPC // d + 128 for d in GD]
    LQ = [TPC // d for d in GD]
    qkvs = [[nc.dram_tensor("qkv%d_%d" % (t, g), [128, 4 * GD[g] * (LQ[g] if t == 0 else LH[g])], BF16, kind=dk).ap() for g in range(3)] for t in range(3)]
    pkgs = [nc.dram_tensor("pkg%d" % i, [PKG_ROWS, 256], BF16, kind="Internal").ap() for i in range(2)]
    gats = [nc.dram_tensor("gat%d" % i, [NCORE * PKG_ROWS, 256], BF16, kind="Internal").ap() for i in range(2)]

    with ExitStack() as stack:
        arena = stack.enter_context(nc.sbuf_tensor("arena", [128, 103 * 1024 + 384], BF16))
        ps = stack.enter_context(nc.psum_tensor("ps", [128, 4096], F32))
        P = Prog(nc, stack)
        cc_sem = stack.enter_context(nc.semaphore("cc_sem"))
        P.sems["cc"] = cc_sem

        state = {"off": 0}

        def alloc(nbytes):
            o = state["off"]
            state["off"] = o + ((nbytes + 63) // 64) * 64
            assert state["off"] <= (103 * 1024 + 384) * 2, state["off"]
            return o

        def view(off, nbytes, dt=BF16):
            a = arena[:, off // 2:(off + nbytes) // 2]
            if dt is not BF16:
                a = a.bitcast(dt)
            return a

        o_cf = alloc(NCF * 4)
        cf = view(o_cf, NCF * 4, F32)
        o_cb = alloc(NCB * 2)
        cb = view(o_cb, NCB * 2)
        o_misc = alloc(128 * 2 * 3 + 2048)
        ones = view(o_misc, 256)
        ones_f = view(o_misc + 256, 256)
        ones_l = view(o_misc + 512, 256)
        mask4 = view(o_misc + 768, 2048)
        o_uh = alloc(4096)
        uhalo = view(o_uh, 4096).rearrange("p (a c) -> p a c", a=2)
        o_km = alloc(8192)
        kmT = view(o_km, 4096).rearrange("p (a c) -> p a c", a=8)
        vm = view(o_km + 4096, 4096).rearrange("p (a c) -> p a c", a=2)
        R_const = P.res("const")
        R_uh = P.res("uhalo")
        R_km = P.res("kmvm")
        phase_mark = state["off"]

        ident = cb[:, CB_ID:CB_ID + 128]

        PB = [P.res("pb%d" % i) for i in range(8)]
        pbi = {"i": 0}

        STB = 7

        def bank():
            i = pbi["i"]
            pbi["i"] = (i + 1) % 7
            return PB[i], ps[:, i * 512:(i + 1) * 512]

        def bank2():
            i = pbi["i"]
            if i % 2:
                i += 1
            if i >= 6:
                i = 0
            pbi["i"] = (i + 2) % 7
            return [PB[i], PB[i + 1]], ps[:, i * 512:(i + 2) * 512]

        def mm(bres, out_ap, pairs, reads):
            def fn(h):
                n = len(pairs)
                ins = None
                for i, (l, r) in enumerate(pairs):
                    ins = h.matmul(out_ap, lhsT=l, rhs=r, start=(i == 0), stop=(i == n - 1))
                return ins
            P.op("pe", reads, bres if isinstance(bres, list) else [bres], fn)

        blk_tok = {}
        NSLOT = 3
        wsl = {"n": 0, "sched": [], "pos": 0, "issued": 0, "sb": {}}

        def setup_wslots():
            offs = [alloc(BLK * 2) for _ in range(NSLOT)]
            wsl["ap"] = [view(o, BLK * 2) for o in offs]
            wsl["res"] = [P.res("wslot%d" % i) for i in range(NSLOT)]
            wsl["res_sw"] = [P.res("wslotsw%d" % i) for i in range(NSLOT)]

        def _issue_next():
            i = wsl["issued"]
            if i >= len(wsl["sched"]):
                return
            blk = wsl["sched"][i]
            s = i % NSLOT
            r = wsl["res"][s]
            used = blk_used(blk)
            if blk in blk_tok:
                P._wait("sp", [blk_tok[blk]])
                P.dma("sp", [], [r], r, wsl["ap"][s][:, 0:used], wbf_rows(blk, blk + 1)[:, 0:used])
            else:
                P.dma("pool", [], [r], wsl["res_sw"][s], wsl["ap"][s][:, 0:used], wall[blk * 128:(blk + 1) * 128, 0:used])
                if not (B_MEMKV <= blk < B_MEMKV + 4):
                    wsl["sb"][i] = blk
            wsl["issued"] = i + 1

        def wstart(sched):
            wsl["sched"] = sched
            wsl["pos"] = 0
            wsl["issued"] = 0
            wsl["sb"] = {}
            for _ in range(NSLOT - 1):
                _issue_next()

        def wnext(expect):
            i = wsl["pos"]
            assert wsl["sched"][i] == expect, (i, wsl["sched"][i], expect)
            s = i % NSLOT
            if i in wsl["sb"]:
                blk = wsl["sb"].pop(i)
                r = wsl["res"][s]
                used = blk_used(blk)
                P.dma("sp", [r], [], r, wbf_rows(blk, blk + 1)[:, 0:used], wsl["ap"][s][:, 0:used])
                blk_tok[blk] = (r.dkey, r.dcount)
            while wsl["issued"] < min(i + NSLOT, len(wsl["sched"])):
                _issue_next()
            wsl["pos"] = i + 1
            return wsl["res"][s], wsl["ap"][s]

        def setup_common(ncp=64):
            d = {}
            o = alloc(KC * T * 4)
            d["xs"] = view(o, KC * T * 4, F32).rearrange("p (k t) -> p k t", k=KC)
            d["XS"] = [P.res("xs%d" % k) for k in range(KC)]
            o = alloc(ncp * T * 2)
            d["cp"] = view(o, ncp * T * 2).rearrange("p (k t) -> p k t", k=ncp)
            d["CP"] = [P.res("cp%d" % k) for k in range(ncp)]
            o = alloc(T * 4)
            d["rstd"] = view(o, T * 4, F32)
            d["Rrstd"] = P.res("rstd")
            d["tmp"] = []
            d["Rtmp"] = []
            for i in range(4):
                o = alloc(T * 4)
                d["tmp"].append(view(o, T * 4, F32))
                d["Rtmp"].append(P.res("tmp%d" % i))
            d["ti"] = 0
            return d

        def tmpf(cm):
            i = cm["ti"]
            cm["ti"] = (i + 1) % 4
            return cm["Rtmp"][i], cm["tmp"][i]

        def stats_sq(cm, k, ncols=T):
            xs, XS, cp, CP = cm["xs"], cm["XS"], cm["cp"], cm["CP"]
            sl = slice(0, ncols)
            P.op("act", [XS[k]], [CP[k]], lambda h: h.activation(out=cp[:, k, sl], in_=xs[:, k, sl], func=AF.Square))

        def stats_mm(cm, k, ncols=T):
            cp, CP = cm["cp"], cm["CP"]
            sl = slice(0, ncols)
            P.op("pe", [CP[k], R_const], [PB[STB]], lambda h: h.matmul(ps[:, STB * 512:STB * 512 + ncols], lhsT=ones, rhs=cp[:, k, sl], start=(k == 0), stop=(k == KC - 1)))

        def stats_chunk(cm, k, ncols=T):
            stats_sq(cm, k, ncols)
            stats_mm(cm, k, ncols)

        def stats_delayed(cm, n, lag=3):
            stats_sq(cm, n)
            if n - lag >= 0:
                stats_mm(cm, n - lag)
            if n == KC - 1:
                for k in range(max(0, KC - lag), KC):
                    stats_mm(cm, k)

        def norm_finish(cm, gcol, dst_fn, ncols=T, after_k=None):
            xs, XS = cm["xs"], cm["XS"]
            sl = slice(0, ncols)
            rstd, Rr = cm["rstd"], cm["Rrstd"]
            bap = ps[:, STB * 512:STB * 512 + ncols]
            P.op("act", [PB[STB], R_const], [Rr], lambda h: h.activation(out=rstd[:, sl], in_=bap, func=AF.Sqrt, scale=1.0 / D, bias=cf[:, C_EPS:C_EPS + 1]))
            P.op("dve", [], [Rr], lambda h: h.reciprocal(out=rstd[:, sl], in_=rstd[:, sl]))
            for k in range(KC):
                dr, dap = dst_fn(k)
                P.op("dve", [XS[k], Rr, R_const], [dr],
                     lambda h: h.scalar_tensor_tensor(out=dap, in0=xs[:, k, sl], scalar=cf[:, gcol + k:gcol + k + 1], in1=rstd[:, sl], op0=ALU.mult, op1=ALU.mult))
                if after_k is not None:
                    after_k(k)

        def rmsnorm(cm, gcol, dst_fn, ncols=T):
            for k in range(KC):
                stats_chunk(cm, k, ncols)
            norm_finish(cm, gcol, dst_fn, ncols)

        def ffn(cm, b_base, HID0):
            xs, XS, cp, CP = cm["xs"], cm["XS"], cm["cp"], cm["CP"]
            for half in range(2):
                for j in range(11):
                    jb = half * 11 + j
                    wr, wap = wnext(b_base + half * 19 + j)
                    w3 = wap.rearrange("p (k c) -> p k c", k=KC)
                    for sub in range(2):
                        gr, gap = bank()
                        mm(gr, gap, [(w3[:, k, sub * 128:(sub + 1) * 128], cp[:, k, :]) for k in range(KC)], [wr] + CP[0:KC])
                        ur, uap = bank()
                        mm(ur, uap, [(w3[:, k, 256 + sub * 128:256 + (sub + 1) * 128], cp[:, k, :]) for k in range(KC)], [wr] + CP[0:KC])
                        tr, tap = tmpf(cm)
                        P.op("act", [gr], [tr], lambda h, gap=gap, tap=tap: h.activation(out=tap, in_=gap, func=AF.Silu))
                        hc = HID0 + j * 2 + sub
                        P.op("dve", [tr, ur], [CP[hc]], lambda h, tap=tap, uap=uap, hc=hc: h.tensor_tensor(out=cp[:, hc, :], in0=uap, in1=tap, op=ALU.mult))
                for n2 in range(8):
                    wr, wap = wnext(b_base + half * 19 + 11 + n2)
                    w3 = wap[:, 0:HF * 256].rearrange("p (k c) -> p k c", k=HF)
                    for sub in range(2):
                        n = n2 * 2 + sub
                        yr, yap = bank()
                        mm(yr, yap, [(w3[:, k, sub * 128:(sub + 1) * 128], cp[:, HID0 + k, :]) for k in range(HF)], [wr] + CP[HID0:HID0 + HF])
                        P.op("dve", [yr], [XS[n]], lambda h, yap=yap, n=n: h.scalar_tensor_tensor(out=xs[:, n, :], in0=yap, scalar=0.5, in1=xs[:, n, :], op0=ALU.mult, op1=ALU.add))
                        if half == 1:
                            stats_delayed(cm, n)

        P.dma("sp", [], [R_const], R_const, cf, cfd)
        o_tmpcb = alloc(NCB * 4)
        cbtmp = view(o_tmpcb, NCB * 4, F32)
        R_cbt = P.res("cbtmp")
        P.dma("sp", [], [R_cbt], R_cbt, cbtmp, cbd)
        P.op("dve", [R_cbt], [R_const], lambda h: h.tensor_copy(out=cb, in_=cbtmp))
        P.op("dve", [], [R_const], lambda h: h.memset(ones, 1.0))
        P.op("dve", [], [R_const], lambda h: h.memset(ones_f, 1.0))
        P.op("dve", [], [R_const], lambda h: h.memset(ones_l, 1.0))
        P.op("dve", [], [R_const], lambda h: h.tensor_scalar(out=ones_f[0:64, :], in0=ones_f[0:64, :], scalar1=cf[0:64, C_VPREV:C_VPREV + 1], scalar2=None, op0=ALU.mult))
        P.op("dve", [], [R_const], lambda h: h.tensor_scalar(out=ones_l[64:128, :], in0=ones_l[64:128, :], scalar1=cf[64:128, C_VNEXT:C_VNEXT + 1], scalar2=None, op0=ALU.mult))
        m4 = mask4.rearrange("p (h a q) -> p h a q", h=4, a=2)
        for hh in range(4):
            P.op("dve", [], [R_const], lambda h, hh=hh: h.tensor_copy(out=m4[:, hh, :, :], in_=cb[:, CB_MA:CB_MA + 256].rearrange("p (a q) -> p a q", a=2)))
        P.barrier()
        state["off"] = phase_mark

        setup_wslots()
        cm = setup_common(38)
        xs, XS, cp, CP = cm["xs"], cm["XS"], cm["cp"], cm["CP"]
        HID0 = 16
        o = alloc(T * 4)
        posi = view(o, T * 4, I32)
        R_pos = P.res("pos")
        o = alloc(T * 4)
        cosT = view(o, T * 4, F32)
        o = alloc(T * 4)
        sinT = view(o, T * 4, F32)
        R_cs = P.res("cossin")
        o = alloc(T * 4)
        tint = view(o, T * 4, I32)
        R_ti = P.res("tint")
        stg = []
        Rstg = []
        for i in range(3):
            o = alloc(4 * T * 2)
            stg.append(view(o, 4 * T * 2))
            Rstg.append(P.res("stg%d" % i))
        o = alloc(4 * 1024 * 2)
        ustage = view(o, 4 * 1024 * 2).rearrange("p (a c) -> p a c", a=4)
        R_us = P.res("ustage")
        R_x1st = P.res("x1store")
        o = alloc(4096 * 2)
        ztile = view(o, 4096 * 2)
        R_z = P.res("ztile")
        P.op("dve", [], [R_z], lambda h: h.memset(ztile, 0.0))
        for t in (1, 2):
            for g in range(3):
                d = GD[g]
                dstz = qkvs[t][g].rearrange("p (h r l) -> p h r l", h=4, r=d)
                z4 = ztile[:, 0:4 * d * 64].rearrange("p (h r l) -> p h r l", h=4, r=d)
                for l0 in (0, LH[g] - 64):
                    if d == 16:
                        for hh in range(4):
                            P.dma("act", [R_z], [], R_z, dstz[:, hh, :, l0:l0 + 64], z4[:, hh, :, :])
                    else:
                        P.dma("act", [R_z], [], R_z, dstz[:, :, :, l0:l0 + 64], z4)

        R_pk = [P.res("pack0"), P.res("pack1")]

        def pack_pkg(ht):
            P._wait("pool", [(r.dkey, r.dcount) for r in Rstg + [R_us] if r.dcount])
            pkf = pkgs[ht].rearrange("r c -> (r c)")
            for t in range(2):
                for g in range(3):
                    d = GD[g]
                    src = qkvs[1 + t][g].rearrange("p (h r l) -> p h r l", h=4, r=d)
                    l0 = 64 if ht == 0 else LH[g] - 128
                    o0 = (t * 21 + PKG_RR0[g]) * PKG_G
                    dstv = pkf[o0:o0 + d * PKG_G].rearrange("(p h r l) -> p h r l", p=128, h=4, r=d)
                    for hh in range(4):
                        P.dma("pool", [], [], R_pk[ht], dstv[:, hh, :, :], src[:, hh, :, l0:l0 + 64])
            ut0 = 0 if ht == 0 else 15
            P.dma("pool", [], [], R_pk[ht], pkf[PKG_U:PKG_U + 131072].rearrange("(p c) -> p c", p=128), us[ut0 * 128:(ut0 + 1) * 128, :])

        def send_pkg(ht):
            P._wait("pool", [(R_pk[ht].dkey, R_pk[ht].dcount)])
            nc.gpsimd.collective_compute("AllGather", ALU.bypass, replica_groups=[list(range(NCORE))], ins=[pkgs[ht]], outs=[gats[ht]]).then_inc(cc_sem, 1)

        sched1 = []
        for m in range(NT):
            sched1 += [B_FFN1 + i for i in range(38)]
            sched1 += [B_WIN1 + i for i in range(11)]
        wstart(sched1)
        xT3 = xT.rearrange("(k p) t -> p k t", p=128)
        x1s3 = x1s.rearrange("(k p) t -> p k t", p=128)
        outT3 = outT.rearrange("(k p) t -> p k t", p=128)
        stgi = 0
        for m in range(NT if STOP_AFTER >= 1 else 0):
            tsl = slice(m * T, (m + 1) * T)
            if m == 3:
                send_pkg(0)
            for k in range(KC):
                P.dma("sp", [], [XS[k]], XS[k], xs[:, k, :], xT3[:, k, tsl])
            rmsnorm(cm, C_GAIN + 0, lambda k: (CP[k], cp[:, k, :]))
            ffn(cm, B_FFN1, HID0)
            P.dma("pool", XS, [], R_x1st, x1s3[:, :, tsl], xs)
            norm_finish(cm, C_GAIN + 16, lambda k: (CP[k], cp[:, k, :]))
            t0r, t0 = tmpf(cm)
            t1r, t1 = tmpf(cm)
            P.dma("sp", [], [R_pos], R_pos, posi, posd[:, tsl])
            P.op("dve", [R_pos], [t0r], lambda h: h.tensor_copy(out=t0, in_=posi))
            P.op("dve", [R_const], [t0r], lambda h: h.tensor_scalar(out=t0, in0=t0, scalar1=cf[:, C_INVF:C_INVF + 1], scalar2=None, op0=ALU.mult))
            P.op("dve", [t0r], [R_ti], lambda h: h.tensor_copy(out=tint, in_=t0))
            P.op("dve", [R_ti], [t1r], lambda h: h.tensor_copy(out=t1, in_=tint))
            P.op("dve", [t1r], [t0r], lambda h: h.tensor_tensor(out=t0, in0=t0, in1=t1, op=ALU.subtract))

            def wrap(buf, bufr, scr, scrr):
                P.op("dve", [bufr], [scrr], lambda h: h.tensor_scalar(out=scr, in0=buf, scalar1=0.5, scalar2=None, op0=ALU.is_gt))
                P.op("dve", [scrr], [bufr], lambda h: h.tensor_tensor(out=buf, in0=buf, in1=scr, op=ALU.subtract))
                P.op("dve", [bufr], [scrr], lambda h: h.tensor_scalar(out=scr, in0=buf, scalar1=-0.5, scalar2=None, op0=ALU.is_lt))
                P.op("dve", [scrr], [bufr], lambda h: h.tensor_tensor(out=buf, in0=buf, in1=scr, op=ALU.add))
            wrap(t0, t0r, t1, t1r)
            P.op("act", [t0r], [R_cs], lambda h: h.activation(out=sinT, in_=t0, func=AF.Sin, scale=6.2831845))
            P.op("dve", [], [t0r], lambda h: h.tensor_scalar(out=t0, in0=t0, scalar1=0.25, scalar2=None, op0=ALU.add))
            wrap(t0, t0r, t1, t1r)
            P.op("act", [t0r], [R_cs], lambda h: h.activation(out=cosT, in_=t0, func=AF.Sin, scale=6.2831845))
            P.op("dve", [R_const], [R_cs], lambda h: h.tensor_scalar(out=sinT, in0=sinT, scalar1=cf[:, C_SSIGN:C_SSIGN + 1], scalar2=None, op0=ALU.mult))
            for bq in range(2):
                wr, wap = wnext(B_WIN1 + bq)
                w3 = wap.rearrange("p (k c) -> p k c", k=KC)
                for tt in range(4):
                    br, bap = bank()
                    mm(br, bap, [(cp[:, k, tt * 128:(tt + 1) * 128], w3[:, k, :]) for k in range(KC)], [wr] + CP[0:KC])
                    P.op("act", [br], [R_us], lambda h, bap=bap, tt=tt, bq=bq: h.activation(out=ustage[:, tt, bq * 512:(bq + 1) * 512], in_=bap, func=AF.Copy))
            P.dma("pool", [R_us], [], R_us, us[m * 512:(m + 1) * 512, :].rearrange("(a p) c -> p a c", p=128), ustage)
            for g in range(3):
                d = GD[g]
                L = T // d
                for t in range(3):
                    wr, wap = wnext(B_WIN1 + 2 + g * 3 + t)
                    w3 = wap.rearrange("p (k c) -> p k c", k=KC)
                    sr, sap = Rstg[stgi % 3], stg[stgi % 3]
                    stgi += 1
                    s4 = sap.rearrange("p (h r l) -> p h r l", h=4, r=d)
                    for hh in range(4):
                        br, bap = bank()
                        mm(br, bap, [(w3[:, k, hh * 128:(hh + 1) * 128], cp[:, k, :]) for k in range(KC)], [wr] + CP[0:KC])
                        bperm = bap.rearrange("p (l r) -> p r l", r=d)
                        if t == 2:
                            P.op("act", [br], [sr], lambda h, bperm=bperm, hh=hh, s4=s4: h.activation(out=s4[:, hh, :, :], in_=bperm, func=AF.Copy))
                        else:
                            ar, aap = tmpf(cm)
                            b2r, b2ap = tmpf(cm)
                            P.op("dve", [br, R_cs], [ar], lambda h, bap=bap, aap=aap: h.tensor_tensor(out=aap, in0=bap, in1=cosT, op=ALU.mult))
                            P.op("dve", [br, R_cs], [b2r], lambda h, bap=bap, b2ap=b2ap: h.tensor_tensor(out=b2ap[0:64, :], in0=bap[64:128, :], in1=sinT[0:64, :], op=ALU.mult))
                            P.op("dve", [br, R_cs], [b2r], lambda h, bap=bap, b2ap=b2ap: h.tensor_tensor(out=b2ap[64:128, :], in0=bap[0:64, :], in1=sinT[64:128, :], op=ALU.mult))
                            P.op("pool", [ar, b2r], [sr], lambda h, aap=aap, b2ap=b2ap, hh=hh, s4=s4, d=d: h.tensor_tensor(
                                out=s4[:, hh, :, :], in0=aap.rearrange("p (l r) -> p r l", r=d), in1=b2ap.rearrange("p (l r) -> p r l", r=d), op=ALU.add))
                    dst = qkvs[t][g].rearrange("p (h r l) -> p h r l", h=4, r=d)
                    off = (0 if t == 0 else 64) + m * L
                    if d == 16:
                        for hh in range(4):
                            P.dma("pool", [sr], [], sr, dst[:, hh, :, off:off + L], s4[:, hh, :, :])
                    else:
                        P.dma("pool", [sr], [], sr, dst[:, :, :, off:off + L], s4)
            if m == 1:
                pack_pkg(0)
        P.barrier()

        if STOP_AFTER >= 2:
            pack_pkg(1)
            send_pkg(1)

        if STOP_AFTER >= 3:
            state["off"] = phase_mark
            o_q = alloc(4 * 2048 * 2)
            o_k = alloc(4 * 4096 * 2)
            o_v = alloc(4 * 4096 * 2)
            o_num = alloc(4 * TPC * 4)
            o_den = alloc(4 * TPC * 4)
            numacc = view(o_num, 4 * TPC * 4, F32)
            denacc = view(o_den, 4 * TPC * 4, F32)
            R_acc = P.res("acc")
            R_q, R_k, R_v = P.res("Qt"), P.res("Kt"), P.res("Vt")
            cand = []
            Rcand = []
            for i in range(2):
                o = alloc(4 * 16 * 64 * 2)
                cand.append(view(o, 4 * 16 * 64 * 2))
                Rcand.append(P.res("cand%d" % i))
            Pt = []
            RPt = []
            for i in range(2):
                o = alloc(1024 * 2)
                Pt.append(view(o, 1024 * 2))
                RPt.append(P.res("Pt%d" % i))
            Vtok = []
            RVt = []
            for i in range(4):
                o = alloc(512 * 2)
                Vtok.append(view(o, 512 * 2).rearrange("p (h c) -> p h c", h=4))
                RVt.append(P.res("Vtok%d" % i))
            o = alloc(16 * 128 * 2)
            dg = view(o, 16 * 128 * 2).rearrange("p (a c) -> p a c", a=16)
            R_dg = P.res("diag")
            for side in range(2):
                for s in range(NCORE):
                    fcol = (C_FPREV if side == 0 else C_FNEXT) + s
                    P.op("dve", [R_const], [R_dg], lambda h, side=side, s=s, fcol=fcol: h.tensor_scalar(out=dg[:, side * 8 + s, :], in0=ident, scalar1=cf[:, fcol:fcol + 1], scalar2=None, op0=ALU.mult))
            gatfs = [gt_.rearrange("r c -> (r c)") for gt_ in gats]
            cstate = {"ci": 0}

            def select(side, ncols, src_fn, dst_fn, dst_res):
                nch = (ncols + 511) // 512
                need = 2 if side == 0 else 1
                if cstate.get("ccw", 0) < need:
                    P.eng["sp"]["h"].wait_ge(cc_sem, need)
                    cstate["ccw"] = need
                for s in range(NCORE):
                    cr, cap = Rcand[cstate["ci"] % 2], cand[cstate["ci"] % 2]
                    cstate["ci"] += 1
                    P.dma("sp", [], [cr], cr, cap[:, 0:ncols], src_fn(s))
                    for c in range(nch):
                        n = min(512, ncols - c * 512)
                        P.op("pe", [cr, R_dg], [PB[c]], lambda h, c=c, n=n, s=s, cap=cap: h.matmul(
                            ps[:, c * 512:c * 512 + n], lhsT=dg[:, side * 8 + s, :], rhs=cap[:, c * 512:c * 512 + n], start=(s == 0), stop=(s == NCORE - 1)))
                for c in range(nch):
                    n = min(512, ncols - c * 512)
                    if c % 2 == 0:
                        P.op("act", [PB[c]], [dst_res], lambda h, c=c, n=n: h.activation(out=dst_fn(c * 512, n), in_=dst_shape(ps[:, c * 512:c * 512 + n]), func=AF.Copy))
                    else:
                        P.op("dve", [PB[c]], [dst_res], lambda h, c=c, n=n: h.tensor_copy(out=dst_fn(c * 512, n), in_=dst_shape(ps[:, c * 512:c * 512 + n])))
                pbi["i"] = 0

            def u_halo(side):
                select(side, 1024,
                       lambda s, side=side: gatfs[1 - side][s * PKG_N + PKG_U: s * PKG_N + PKG_U + 131072].rearrange("(p c) -> p c", p=128),
                       lambda c0, n, side=side: uhalo[:, side, c0:c0 + n], R_uh)
            pti = 0
            unit_ctr = {"n": 0}
            vctr = {"n": 0}
            vowner = {}
            for g in range(3):
                d = GD[g]
                Lh = LH[g]
                Lq = TPC // d
                nq = Lq // 128
                Qt = view(o_q, 4 * d * Lq * 2).rearrange("p (h r l) -> p h r l", h=4, r=d)
                Kt = view(o_k, 4 * d * Lh * 2).rearrange("p (h r l) -> p h r l", h=4, r=d)
                Vt = view(o_v, 4 * d * Lh * 2).rearrange("p (h r l) -> p h r l", h=4, r=d)
                P.dma("sp", [], [R_q], R_q, view(o_q, 4 * d * Lq * 2), qkvs[0][g])
                P.dma("sp", [], [R_k], R_k, view(o_k, 4 * d * Lh * 2), qkvs[1][g])
                P.dma("sp", [], [R_v], R_v, view(o_v, 4 * d * Lh * 2), qkvs[2][g])
                def halo_side(side):
                    for t in range(2):
                        Xt, Rx = (Kt, R_k) if t == 0 else (Vt, R_v)
                        l0 = 0 if side == 0 else Lh - 64
                        dsth = Xt[:, :, :, l0:l0 + 64].rearrange("p h r l -> p (h r) l")
                        ht = 1 - side
                        o0 = (t * 21 + PKG_RR0[g]) * PKG_G
                        select(side, 4 * d * 64,
                               lambda s, o0=o0, d=d, ht=ht: gatfs[ht][s * PKG_N + o0: s * PKG_N + o0 + d * PKG_G].rearrange("(p c) -> p c", p=128),
                               lambda c0, n, dsth=dsth: dsth[:, c0 // 64:(c0 + n) // 64, :], Rx)
                na4 = numacc.rearrange("p (h l r) -> p h l r", h=4, r=d)
                da4 = denacc.rearrange("p (h l r) -> p h l r", h=4, r=d)

                def stageA(r, jq, use_cache=True):
                    def get_vtok(kt):
                        key = (g, r, kt)
                        if use_cache and key in vt_of and vowner.get(vt_of[key]) == key:
                            return vt_of[key]
                        i = vctr["n"] % 4
                        vctr["n"] += 1
                        vowner[i] = key
                        br, bap = bank()
                        b3 = bap.rearrange("p (h c) -> p h c", h=4)

                        def fn(h):
                            ins = None
                            for hh in range(4):
                                ins = h.matmul(b3[:, hh, :], lhsT=Vt[:, hh, r, kt * 128:(kt + 1) * 128], rhs=ident, start=True, stop=True)
                            return ins
                        P.op("pe", [R_v, R_const], [br], fn)
                        P.op("act", [br], [RVt[i]], lambda h: h.activation(out=Vtok[i], in_=b3, func=AF.Copy))
                        vt_of[key] = i
                        return i
                    unit_ctr["n"] += 1
                    va = get_vtok(jq)
                    vb = get_vtok(jq + 1)
                    brs, bap = bank2()
                    b4 = bap.rearrange("p (h a q) -> p h a q", h=4, a=2)

                    def fnS(h):
                        ins = None
                        for hh in range(4):
                            for ab in range(2):
                                ins = h.matmul(b4[:, hh, ab, :], lhsT=Kt[:, hh, r, (jq + ab) * 128:(jq + ab + 1) * 128], rhs=Qt[:, hh, r, jq * 128:(jq + 1) * 128], start=True, stop=True)
                        return ins
                    P.op("pe", [R_k, R_q], brs, fnS)
                    k_ = unit_ctr["n"] % 2
                    pr, pap = RPt[k_], Pt[k_]
                    P.op("act", brs, [pr], lambda h: h.activation(out=pap, in_=bap, func=AF.Exp, scale=float(128 ** -0.5)))
                    P.op("pool", [R_const], [pr], lambda h: h.tensor_tensor(out=pap, in0=pap, in1=mask4, op=ALU.mult))
                    return dict(r=r, jq=jq, va=va, vb=vb, pr=pr, pap=pap)

                def stageB(cx):
                    r, jq, va, vb, pr, pap = cx["r"], cx["jq"], cx["va"], cx["vb"], cx["pr"], cx["pap"]
                    p4 = pap.rearrange("p (h a q) -> p h a q", h=4, a=2)
                    nr, nap = bank()
                    n3 = nap.rearrange("p (h q) -> p h q", h=4)

                    def fnN(h):
                        ins = None
                        for hh in range(4):
                            ins = h.matmul(n3[:, hh, :], lhsT=Vtok[va][:, hh, :], rhs=p4[:, hh, 0, :], start=True, stop=False)
                            ins = h.matmul(n3[:, hh, :], lhsT=Vtok[vb][:, hh, :], rhs=p4[:, hh, 1, :], start=False, stop=True)
                        return ins
                    P.op("pe", [pr, RVt[va], RVt[vb]], [nr], fnN)
                    dr_, dap = bank()
                    d3 = dap.rearrange("p (h q) -> p h q", h=4)
                    oa = ones_f if jq == 0 else ones
                    ob = ones_l if jq + 1 == nq else ones

                    def fnD(h):
                        h.matmul(d3, lhsT=oa, rhs=p4[:, :, 0, :], start=True, stop=False)
                        return h.matmul(d3, lhsT=ob, rhs=p4[:, :, 1, :], start=False, stop=True)
                    P.op("pe", [pr, R_const], [dr_], fnD)
                    nsl = na4[:, :, jq * 128:(jq + 1) * 128, r]
                    dsl = da4[:, :, jq * 128:(jq + 1) * 128, r]
                    if g == 0:
                        P.op("dve", [nr, R_acc], [], lambda h: h.tensor_copy(out=nsl, in_=n3))
                        P.op("act", [dr_, R_acc], [], lambda h: h.activation(out=dsl, in_=d3, func=AF.Copy))
                    else:
                        P.op("dve", [nr, R_acc], [], lambda h: h.tensor_tensor(out=nsl, in0=n3, in1=nsl, op=ALU.add))
                        P.op("dve", [dr_, R_acc], [], lambda h: h.tensor_tensor(out=dsl, in0=d3, in1=dsl, op=ALU.add))

                pend = None
                vt_of = {}
                dst_shape = lambda a: a.rearrange("p (a l) -> p a l", l=64)
                halo_side(1)
                if g == 0:
                    dst_shape = lambda a: a
                    u_halo(1)
                for (r, jq) in [(r, jq) for r in range(d) for jq in range(1, nq)]:
                    cx = stageA(r, jq)
                    if pend is not None:
                        stageB(pend)
                    pend = cx
                dst_shape = lambda a: a.rearrange("p (a l) -> p a l", l=64)
                halo_side(0)
                if g == 0:
                    dst_shape = lambda a: a
                    u_halo(0)
                for (r, jq) in [(r, 0) for r in range(d)]:
                    cx = stageA(r, jq, use_cache=False)
                    if pend is not None:
                        stageB(pend)
                    pend = cx
                stageB(pend)
                P.op("dve", [R_q, R_k, R_v], [R_acc], lambda h: h.memset(cand[0][:, 0:2], 0.0))
                P.op("dve", [], [R_q, R_k, R_v, Rcand[0]], lambda h: h.memset(cand[0][:, 0:2], 0.0))
            aost = view(o_k, 4 * TPC * 2)
            P.op("dve", [], [R_acc], lambda h: h.reciprocal(out=denacc, in_=denacc))
            P.op("dve", [], [R_acc, R_k], lambda h: h.tensor_tensor(out=aost, in0=numacc, in1=denacc, op=ALU.mult))
            P.dma("sp", [R_k], [], R_k, aos, aost)
            P.barrier()

        if STOP_AFTER >= 4:
            state["off"] = phase_mark
            setup_wslots()
            cm = setup_common()
            xs, XS, cp, CP = cm["xs"], cm["XS"], cm["cp"], cm["CP"]
            HID0 = 16
            QM0, PL0, MX0, MG0, OM0 = 16, 24, 32, 40, 56
            UT0 = 40
            gt = []
            Rgt = []
            for i in range(3):
                o = alloc(T * 4)
                gt.append(view(o, T * 4, F32))
                Rgt.append(P.res("gt%d" % i))
            o = alloc(4 * T * 2)
            aoT = view(o, 4 * T * 2).rearrange("p (h t) -> p h t", h=4)
            R_ao = P.res("aoT")
            o = alloc(2 * T * 2)
            Pm = view(o, 2 * T * 2).rearrange("p (a t) -> p a t", a=2)
            R_pm = P.res("Pm")
            R_out = P.res("outst")

            sched3 = [B_MEMKV + i for i in range(4)]
            for m in range(NT):
                sched3 += [B_QMEM, B_QMEM + 1, B_POOLW]
                sched3 += [B_GU3 + i for i in range(32)]
                sched3 += [B_WOUT + i for i in range(4)]
                sched3 += [B_FFN2 + i for i in range(38)]
            wstart(sched3)
            memT3 = memTd.rearrange("(k p) t -> p k t", p=128)
            P.dma("sp", [], XS, XS[0], xs[:, :, 0:256], memT3)
            rmsnorm(cm, C_GAIN + 64, lambda k: (CP[k], cp[:, k, 0:256]), ncols=256)
            for j in range(2):
                wr, wap = wnext(B_MEMKV + j)
                w3 = wap.rearrange("p (k c) -> p k c", k=KC)
                for q4 in range(4):
                    br, bap = bank()
                    mm(br, bap[:, 0:256], [(w3[:, k, q4 * 128:(q4 + 1) * 128], cp[:, k, 0:256]) for k in range(KC)], [wr] + CP[0:KC])
                    P.op("act", [br], [R_km], lambda h, bap=bap, j=j, q4=q4: h.activation(out=kmT[:, j * 4 + q4, :], in_=bap[:, 0:256], func=AF.Copy))
            for j in range(2):
                wr, wap = wnext(B_MEMKV + 2 + j)
                w3 = wap.rearrange("p (k c) -> p k c", k=KC)
                for mt in range(2):
                    br, bap = bank()
                    mm(br, bap, [(cp[:, k, mt * 128:(mt + 1) * 128], w3[:, k, :]) for k in range(KC)], [wr] + CP[0:KC])
                    P.op("act", [br], [R_km], lambda h, bap=bap, j=j, mt=mt: h.activation(out=vm[:, mt, j * 512:(j + 1) * 512], in_=bap, func=AF.Copy))
            bands = cb[:, CB_BAND:CB_BAND + 4608].rearrange("p (w v n t) -> p w v n t", w=4, v=3, n=3)
            icnt = cf[:, C_ICNT:C_ICNT + 1536].rearrange("p (w v t) -> p w v t", w=4, v=3)
            for m in range(NT):
                tsl = slice(m * T, (m + 1) * T)
                for k in range(KC):
                    P.dma("sp", [], [XS[k]], XS[k], xs[:, k, :], x1s3[:, k, tsl])
                P.dma("sp", [], [R_ao], R_ao, aoT, aos.rearrange("p (h t) -> p h t", h=4)[:, :, tsl])
                lo = max(0, 4 * m - 1)
                hi = min(15, 4 * m + 4)
                s0 = lo - (4 * m - 1)
                nt_ = hi - lo + 1
                ut = cp[:, UT0:UT0 + 12, :].rearrange("p (a b) t -> p a (b t)", a=6)
                P.dma("sp", [], CP[UT0 + 2 * s0:UT0 + 2 * (s0 + nt_)], CP[UT0], ut[:, s0:s0 + nt_, :], us[lo * 128:(hi + 1) * 128, :].rearrange("(a p) c -> p a c", p=128))
                rmsnorm(cm, C_GAIN + 16, lambda k: (CP[k], cp[:, k, :]))
                for j in range(2):
                    wr, wap = wnext(B_QMEM + j)
                    w3 = wap.rearrange("p (k c) -> p k c", k=KC)
                    for q4 in range(4):
                        br, bap = bank()
                        mm(br, bap, [(w3[:, k, q4 * 128:(q4 + 1) * 128], cp[:, k, :]) for k in range(KC)], [wr] + CP[0:KC])
                        c = QM0 + j * 4 + q4
                        P.op("act", [br], [CP[c]], lambda h, bap=bap, c=c: h.activation(out=cp[:, c, :], in_=bap, func=AF.Copy))
                for hm in range(4):
                    brs, bap = bank2()
                    b3 = bap.rearrange("p (a t) -> p a t", a=2)

                    def fnS(h, hm=hm, b3=b3):
                        ins = None
                        for mt in range(2):
                            for c2 in range(2):
                                ins = h.matmul(b3[:, mt, :], lhsT=kmT[:, hm * 2 + c2, mt * 128:(mt + 1) * 128], rhs=cp[:, QM0 + hm * 2 + c2, :], start=(c2 == 0), stop=(c2 == 1))
                        return ins
                    P.op("pe", [R_km, CP[QM0 + hm * 2], CP[QM0 + hm * 2 + 1]], brs, fnS)
                    P.op("act", brs, [R_pm], lambda h, b3=b3: h.activation(out=Pm, in_=b3, func=AF.Exp, scale=float(256 ** -0.5)))
                    dr_, dap = bank()
                    mm(dr_, dap, [(ones, Pm[:, mt, :]) for mt in range(2)], [R_pm, R_const])
                    rr, rap = tmpf(cm)
                    P.op("dve", [dr_], [rr], lambda h, dap=dap, rap=rap: h.reciprocal(out=rap, in_=dap))
                    for c2 in range(2):
                        orr, oap = bank()
                        mm(orr, oap, [(vm[:, mt, hm * 256 + c2 * 128:hm * 256 + (c2 + 1) * 128], Pm[:, mt, :]) for mt in range(2)], [R_pm, R_km])
                        c = OM0 + hm * 2 + c2
                        P.op("dve", [orr, rr], [CP[c]], lambda h, oap=oap, rap=rap, c=c: h.tensor_tensor(out=cp[:, c, :], in0=oap, in1=rap, op=ALU.mult))
                for tt in range(4):
                    i_t = 4 * m + tt
                    var = 0 if i_t == 0 else (2 if i_t == 15 else 1)
                    brs, bap = bank2()
                    b4 = bap.rearrange("p (w c t) -> p w c t", w=4, c=2)

                    def usrc(nb, ch, i_t=i_t, m=m):
                        ti = i_t + nb - 1
                        if ti < 0:
                            return uhalo[:, 0, ch * 128:(ch + 1) * 128], R_uh
                        if ti > 15:
                            return uhalo[:, 1, ch * 128:(ch + 1) * 128], R_uh
                        sl_ = ti - (4 * m - 1)
                        return ut[:, sl_, ch * 128:(ch + 1) * 128], CP[UT0 + 2 * sl_ + (ch // 4)]
                    rd = set()
                    plist = []
                    for wg in range(4):
                        for c2 in range(2):
                            ch = wg * 2 + c2
                            for nb in range(3):
                                ap_, r_ = usrc(nb, ch)
                                rd.add(r_)
                                plist.append((b4[:, wg, c2, :], ap_, bands[:, wg, var, nb, :], nb))

                    def fnP(h, plist=plist):
                        ins = None
                        for (o_, l_, r_, nb) in plist:
                            ins = h.matmul(o_, lhsT=l_, rhs=r_, start=(nb == 0), stop=(nb == 2))
                        return ins
                    P.op("pe", list(rd) + [R_const], brs, fnP)
                    pl = cp[:, PL0:PL0 + 8, tt * 128:(tt + 1) * 128].rearrange("p (w c) t -> p w c t", w=4)
                    for c2 in range(2):
                        P.op("dve", brs + [R_const], CP[PL0:PL0 + 8], lambda h, b4=b4, pl=pl, var=var, c2=c2: h.tensor_tensor(out=pl[:, :, c2, :], in0=b4[:, :, c2, :], in1=icnt[:, :, var, :], op=ALU.mult))
                wr, wap = wnext(B_POOLW)
                pw4 = wap[:, 0:2048].rearrange("p (g k c) -> p g k c", g=4, k=2)
                for wg in range(4):
                    for dc in range(2):
                        br, bap = bank()
                        mm(br, bap, [(pw4[:, wg, k2, dc * 128:(dc + 1) * 128], cp[:, PL0 + wg * 2 + k2, :]) for k2 in range(2)], [wr] + CP[PL0 + wg * 2:PL0 + wg * 2 + 2])
                        c = wg * 2 + dc
                        P.op("act", [br, R_const], [CP[MX0 + c]], lambda h, bap=bap, c=c: h.activation(out=cp[:, MX0 + c, :], in_=bap, func=AF.Identity, scale=cf[:, C_PS + c:C_PS + c + 1]))
                for n2 in range(8):
                    for sub in range(2):
                        c = n2 * 2 + sub
                        wr, wap = wnext(B_GU3 + c * 2)
                        w3 = wap[:, 0:KC * 384].rearrange("p (k c) -> p k c", k=KC)
                        for i in range(3):
                            br, bap = bank()
                            mm(br, bap, [(w3[:, k, i * 128:(i + 1) * 128], cp[:, k, :]) for k in range(KC)], [wr] + CP[0:KC])
                            P.op("act", [br, R_const], [Rgt[i]], lambda h, bap=bap, i=i, c=c: h.activation(out=gt[i], in_=bap, func=AF.Sigmoid, bias=cf[:, C_BG + i * 16 + c:C_BG + i * 16 + c + 1]))
                        ur_, uap_ = wnext(B_GU3 + c * 2 + 1)
                        u3 = uap_[:, 0:20 * 128].rearrange("p (k c) -> p k c", k=20)
                        ucol = slice(0, 128)
                        y0r, y0 = bank()
                        mm(y0r, y0, [(u3[:, k, ucol], cp[:, MX0 + k, :]) for k in range(8)], [ur_] + CP[MX0:MX0 + 8])
                        t0r, t0 = tmpf(cm)
                        P.op("dve", [y0r, Rgt[0]], [t0r], lambda h, y0=y0, t0=t0: h.tensor_tensor(out=t0, in0=y0, in1=gt[0], op=ALU.mult))
                        y1r, y1 = bank()
                        mm(y1r, y1, [(u3[:, 8 + k, ucol], aoT[:, k, :]) for k in range(4)], [ur_, R_ao])
                        t1r, t1 = tmpf(cm)
                        P.op("dve", [y1r, Rgt[1]], [t1r], lambda h, y1=y1, t1=t1: h.tensor_tensor(out=t1, in0=y1, in1=gt[1], op=ALU.mult))
                        P.op("pool", [t1r], [t0r], lambda h, t0=t0, t1=t1: h.tensor_tensor(out=t0, in0=t0, in1=t1, op=ALU.add))
                        y2r, y2 = bank()
                        mm(y2r, y2, [(u3[:, 12 + k, ucol], cp[:, OM0 + k, :]) for k in range(8)], [ur_] + CP[OM0:OM0 + 8])
                        t2r, t2 = tmpf(cm)
                        P.op("dve", [y2r, Rgt[2]], [t2r], lambda h, y2=y2, t2=t2: h.tensor_tensor(out=t2, in0=y2, in1=gt[2], op=ALU.mult))
                        P.op("pool", [t0r, t2r], [CP[MG0 + c]], lambda h, t0=t0, t2=t2, c=c: h.tensor_tensor(out=cp[:, MG0 + c, :], in0=t0, in1=t2, op=ALU.add))
                for j in range(4):
                    wr, wap = wnext(B_WOUT + j)
                    w3 = wap.rearrange("p (k c) -> p k c", k=KC)
                    for q4 in range(4):
                        n = j * 4 + q4
                        br, bap = bank()
                        mm(br, bap, [(w3[:, k, q4 * 128:(q4 + 1) * 128], cp[:, MG0 + k, :]) for k in range(KC)], [wr] + CP[MG0:MG0 + KC])
                        P.op("dve", [br], [XS[n]], lambda h, bap=bap, n=n: h.tensor_tensor(out=xs[:, n, :], in0=bap, in1=xs[:, n, :], op=ALU.add))
                        stats_delayed(cm, n)
                norm_finish(cm, C_GAIN + 32, lambda k: (CP[k], cp[:, k, :]))
                ffn(cm, B_FFN2, HID0)
                norm_finish(cm, C_GAIN + 48, lambda k: (XS[k], xs[:, k, :]),
                            after_k=lambda k: P.dma("pool", [XS[k]], [], R_out, outT3[:, k, tsl], xs[:, k, :]))
        P.barrier()
    return nc


_CACHE = {}


def kernel(**inp):
    inp = {k: np.asarray(v) for k, v in inp.items()}
    wallh = build_wall(inp)
    x = inp["x"][0]
    memT = np.ascontiguousarray(inp["mem"][0].T.astype(np.float32))
    pos = inp["positions"][0].astype(np.int32)
    in_maps = []
    for c in range(NCORE):
        cfh, cbh = build_consts(inp, c)
        xTc = np.ascontiguousarray(x[c * TPC:(c + 1) * TPC, :].T)
        posc = np.ascontiguousarray(np.broadcast_to(pos[c * TPC:(c + 1) * TPC][None, :], (128, TPC)))
        in_maps.append({"xT": xTc, "wall": wallh, "cf": cfh, "cb": cbh, "pos": posc, "memT": memT})
    if "nc" not in _CACHE:
        _CACHE["nc"] = build_program()
    res = run_bass_kernel_spmd(_CACHE["nc"], in_maps, core_ids=list(range(NCORE)))
    _CACHE["res"] = res
    out = np.empty((1, S, D), np.float32)
    for c in range(NCORE):
        out[0, c * TPC:(c + 1) * TPC, :] = res.results[c]["outT"].T
    return out
```
